# Optimizing a Trainium2 kernel written in Bass

```python
import jax, jax.numpy as jnp
from jax import lax
import numpy as np

D_MODEL = 1024
BATCH = 2
SEQ = 8192
DEPTH = 2

CHUNK = 64
N_MIXERS = 2
FOX_HEADS = 8
FOX_HEAD_DIM = D_MODEL // FOX_HEADS
Q_BLOCK = 128
RET_HEADS = 4
RET_QK_DIM = D_MODEL // RET_HEADS
RET_V_DIM = 2 * D_MODEL // RET_HEADS
D_FF = -(-8 * D_MODEL // (3 * 256)) * 256
ROPE_BASE = 10000.0
EPS = 1e-6
N_FOX_LAYERS = (DEPTH + 1) // 2
N_RET_LAYERS = DEPTH // 2

kernel_name = "fox_retnet_interleaved_trunk"


def rmsnorm(x, g):
    xf = x.astype(jnp.float32)
    y = xf * lax.rsqrt(jnp.mean(xf * xf, axis=-1, keepdims=True) + EPS)
    return (y * g.astype(jnp.float32)).astype(x.dtype)


def forgetting_attention(h, w_in, b_f, w_out):
    B, S, _ = h.shape
    H, dh = FOX_HEADS, FOX_HEAD_DIM
    proj = h @ w_in
    q = proj[..., :D_MODEL].reshape(B, S, H, dh).transpose(0, 2, 1, 3)
    k = proj[..., D_MODEL:2 * D_MODEL].reshape(B, S, H, dh).transpose(0, 2, 1, 3)
    v = proj[..., 2 * D_MODEL:3 * D_MODEL].reshape(B, S, H, dh).transpose(0, 2, 1, 3)
    log_f = jax.nn.log_sigmoid((proj[..., 3 * D_MODEL:] + b_f).astype(jnp.float32))
    c = jnp.cumsum(log_f, axis=1).transpose(0, 2, 1)
    nb = S // Q_BLOCK
    qb = q.reshape(B, H, nb, Q_BLOCK, dh).transpose(2, 0, 1, 3, 4)
    cb = c.reshape(B, H, nb, Q_BLOCK).transpose(2, 0, 1, 3)
    pos = jnp.arange(S)
    qpos = pos.reshape(nb, Q_BLOCK)
    scale = dh ** -0.5

    def block(args):
        qi, ci, pi = args
        s = jnp.einsum('bhqd,bhkd->bhqk', qi, k).astype(jnp.float32) * scale
        s = s + (ci[..., :, None] - c[:, :, None, :])
        s = jnp.where(pi[:, None] >= pos[None, :], s, -jnp.inf)
        p = jax.nn.softmax(s, axis=-1).astype(v.dtype)
        return jnp.einsum('bhqk,bhkd->bhqd', p, v)

    o = lax.map(block, (qb, cb, qpos))
    o = o.transpose(1, 0, 3, 2, 4).reshape(B, S, H * dh)
    return o @ w_out


def rotate(t, cos, sin):
    half = t.shape[-1] // 2
    t1, t2 = t[..., :half], t[..., half:]
    return jnp.concatenate([t1 * cos - t2 * sin, t1 * sin + t2 * cos], axis=-1)


def retention(h, w_in, w_out):
    B, S, _ = h.shape
    H, dk, dv, C = RET_HEADS, RET_QK_DIM, RET_V_DIM, CHUNK
    N = S // C
    proj = h @ w_in
    q = proj[..., :D_MODEL].reshape(B, S, H, dk)
    k = proj[..., D_MODEL:2 * D_MODEL].reshape(B, S, H, dk)
    v = proj[..., 2 * D_MODEL:4 * D_MODEL].reshape(B, S, H, dv)
    g = proj[..., 4 * D_MODEL:]
    half = dk // 2
    inv = ROPE_BASE ** (-jnp.arange(half, dtype=jnp.float32) / half)
    ang = jnp.arange(S, dtype=jnp.float32)[:, None] * inv[None, :]
    cos, sin = jnp.cos(ang)[:, None, :], jnp.sin(ang)[:, None, :]
    q = rotate(q.astype(jnp.float32), cos, sin)
    k = rotate(k.astype(jnp.float32), cos, sin) * (dk ** -0.5)
    v = v.astype(jnp.float32)
    log_gamma = jnp.log(jnp.asarray(1.0 - 2.0 ** (-5.0 - np.arange(H)), dtype=jnp.float32))
    idx = jnp.arange(C, dtype=jnp.float32)
    d_intra = jnp.exp(log_gamma[:, None, None] * jnp.abs(idx[:, None] - idx[None, :]))
    q_decay = jnp.exp(log_gamma[:, None] * (idx[None, :] + 1.0))
    k_decay = jnp.exp(log_gamma[:, None] * (C - 1.0 - idx[None, :]))
    chunk_decay = jnp.exp(log_gamma * C)
    to_chunks = lambda t: t.reshape(B, N, C, H, t.shape[-1]).transpose(1, 0, 3, 2, 4)
    qc, kc, vc = to_chunks(q), to_chunks(k), to_chunks(v)

    def step(state, inp):
        qi, ki, vi = inp
        a = jnp.einsum('bhid,bhjd->bhij', qi, ki) * d_intra
        inner = jnp.einsum('bhij,bhjv->bhiv', a, vi)
        cross = jnp.einsum('bhid,bhdv->bhiv', qi, state) * q_decay[None, :, :, None]
        new_state = state * chunk_decay[None, :, None, None] + jnp.einsum(
            'bhjd,bhjv->bhdv', ki * k_decay[None, :, :, None], vi)
        return new_state, inner + cross

    state0 = jnp.zeros((B, H, dk, dv), jnp.float32)
    _, y = lax.scan(step, state0, (qc, kc, vc))
    y = y.transpose(1, 0, 3, 2, 4).reshape(B, S, H, dv)
    mu = jnp.mean(y, axis=-1, keepdims=True)
    var = jnp.mean(jnp.square(y - mu), axis=-1, keepdims=True)
    y = ((y - mu) * lax.rsqrt(var + EPS)).reshape(B, S, H * dv).astype(h.dtype)
    return (jax.nn.silu(g) * y) @ w_out


def swiglu(h, w_in, w_out):
    gu = h @ w_in
    return (jax.nn.silu(gu[..., :D_FF]) * gu[..., D_FF:]) @ w_out


def setup_inputs(seed: int = 0) -> dict:
    key = jax.random.key(seed)
    ks = jax.random.split(key, 12)
    f32 = jnp.float32
    x = jax.random.normal(ks[0], (BATCH, SEQ, D_MODEL), f32)
    norm_mix = 1.0 + 0.02 * jax.random.normal(ks[1], (DEPTH, D_MODEL), f32)
    norm_ffn = 1.0 + 0.02 * jax.random.normal(ks[2], (DEPTH, D_MODEL), f32)
    fox_w_in = jax.random.normal(ks[3], (N_FOX_LAYERS, D_MODEL, 3 * D_MODEL + FOX_HEADS), f32) * D_MODEL ** -0.5
    fox_b_f = 3.0 + 0.5 * jax.random.normal(ks[4], (N_FOX_LAYERS, FOX_HEADS), f32)
    fox_w_out = jax.random.normal(ks[5], (N_FOX_LAYERS, D_MODEL, D_MODEL), f32) * D_MODEL ** -0.5
    ret_w_in = jax.random.normal(ks[6], (N_RET_LAYERS, D_MODEL, 6 * D_MODEL), f32) * D_MODEL ** -0.5
    ret_w_out = jax.random.normal(ks[7], (N_RET_LAYERS, 2 * D_MODEL, D_MODEL), f32) * (2 * D_MODEL) ** -0.5
    ffn_w_in = jax.random.normal(ks[8], (DEPTH, D_MODEL, 2 * D_FF), f32) * D_MODEL ** -0.5
    ffn_w_out = jax.random.normal(ks[9], (DEPTH, D_FF, D_MODEL), f32) * D_FF ** -0.5
    final_norm = 1.0 + 0.02 * jax.random.normal(ks[10], (D_MODEL,), f32)
    return {"x": x, "norm_mix": norm_mix, "norm_ffn": norm_ffn,
            "fox_w_in": fox_w_in, "fox_b_f": fox_b_f, "fox_w_out": fox_w_out,
            "ret_w_in": ret_w_in, "ret_w_out": ret_w_out,
            "ffn_w_in": ffn_w_in, "ffn_w_out": ffn_w_out, "final_norm": final_norm}


def reference(x, norm_mix, norm_ffn, fox_w_in, fox_b_f, fox_w_out, ret_w_in, ret_w_out,
              ffn_w_in, ffn_w_out, final_norm):
    for i in range(DEPTH):
        h = rmsnorm(x, norm_mix[i])
        j = i // N_MIXERS
        if i % N_MIXERS == 0:
            x = x + forgetting_attention(h, fox_w_in[j], fox_b_f[j], fox_w_out[j])
        else:
            x = x + retention(h, ret_w_in[j], ret_w_out[j])
        h = rmsnorm(x, norm_ffn[i])
        x = x + swiglu(h, ffn_w_in[i], ffn_w_out[i])
    return rmsnorm(x, final_norm)
```

```python
import numpy as np
from contextlib import ExitStack, contextmanager
import ml_dtypes
import concourse.bass as bass
import concourse.mybir as mybir
from concourse.bass_utils import run_bass_kernel_spmd

F32 = mybir.dt.float32
BF16 = mybir.dt.bfloat16
AF = mybir.ActivationFunctionType
ALU = mybir.AluOpType

D = 1024
S = 8192
NB = 2
DFF = 2816
EPS = 1e-6
NCORES = 8
NEG = -30000.0

SAME_ENG_SYNC = True


class Eng:
    def __init__(self, name, h, sem):
        self.name = name
        self.h = h
        self.sem = sem
        self.count = 0
        self.seen = {}


class Buf:
    __slots__ = ("name", "w", "r", "dsem", "dcount")

    def __init__(self, name):
        self.name = name
        self.w = {}
        self.r = {}
        self.dsem = None
        self.dcount = 0


class Scope:
    def __init__(self, kb):
        self.kb = kb
        self.stack = ExitStack()

    def sb(self, name, shape, dtype):
        return self.stack.enter_context(self.kb.nc.sbuf_tensor(self.kb.pfx + name, shape, dtype))

    def ps(self, name, shape, dtype):
        return self.stack.enter_context(self.kb.nc.psum_tensor(self.kb.pfx + name, shape, dtype))


class KB:
    def __init__(self, nc):
        self.nc = nc
        self.stack = ExitStack()
        self.engs = {}
        self.semof = {}
        self.dbufs = []
        for name in ["tensor", "vector", "scalar", "gpsimd", "sync"]:
            sem = self.stack.enter_context(nc.semaphore("pg_" + name))
            e = Eng(name, getattr(nc, name), sem)
            self.engs[name] = e
            self.semof[name] = sem
        self.nsem = 0
        self.pfx = ""

    def buf(self, name):
        return Buf(name)

    def bufs(self, name, n):
        return [Buf("%s%d" % (name, i)) for i in range(n)]

    def mkdma(self, b):
        if b.dsem is None:
            b.dsem = self.stack.enter_context(self.nc.semaphore("d%d" % self.nsem))
            self.nsem += 1
            self.semof[("d", id(b))] = b.dsem
            self.dbufs.append(b)

    @contextmanager
    def scope(self, barrier=True):
        sc = Scope(self)
        try:
            yield sc
        finally:
            if barrier:
                self.barrier()
            sc.stack.close()

    def _wait(self, eng, deps):
        for key, val in deps.items():
            if key == eng.name:
                if eng.name in ("tensor", "sync") or not SAME_ENG_SYNC:
                    continue
            if eng.seen.get(key, 0) >= val:
                continue
            eng.h.wait_ge(self.semof[key], val)
            eng.seen[key] = val

    @staticmethod
    def _merge(d, s):
        for k, v in s.items():
            if d.get(k, 0) < v:
                d[k] = v

    def _deps(self, reads, writes, pwrites):
        deps = {}
        for b in reads:
            self._merge(deps, b.w)
        for b in writes:
            self._merge(deps, b.w)
            self._merge(deps, b.r)
        for b in pwrites:
            self._merge(deps, b.r)
        return deps

    def _update(self, key, val, reads, writes, pwrites):
        for b in reads:
            if b.r.get(key, 0) < val:
                b.r[key] = val
        for b in writes:
            b.w = {key: val}
            b.r = {}
        for b in pwrites:
            if b.w.get(key, 0) < val:
                b.w[key] = val

    def op(self, engname, fn, reads=(), writes=(), pwrites=(), track=True):
        eng = self.engs[engname]
        self._wait(eng, self._deps(reads, writes, pwrites))
        ins = fn(eng.h)
        if track:
            eng.count += 1
            ins.then_inc(eng.sem, 1)
            val = eng.count
        else:
            val = eng.count + 1
        self._update(eng.name, val, reads, writes, pwrites)
        return ins

    def dma(self, qname, out, in_, slot, reads=(), writes=(), pwrites=(), **kw):
        q = self.engs[qname]
        self.mkdma(slot)
        self._wait(q, self._deps(reads, writes, pwrites))
        ins = q.h.dma_start(out=out, in_=in_, **kw)
        slot.dcount += 16
        ins.then_inc(slot.dsem, 16)
        self._update(("d", id(slot)), slot.dcount, reads, writes, pwrites)
        return ins

    def coll(self, kind, groups, src, dst, slot, reads=(), writes=(), pwrites=()):
        q = self.engs["gpsimd"]
        self.mkdma(slot)
        self._wait(q, self._deps(reads, writes, pwrites))
        ins = q.h.collective_compute(kind, ALU.bypass, replica_groups=groups, ins=[src], outs=[dst])
        slot.dcount += 1
        ins.then_inc(slot.dsem)
        self._update(("d", id(slot)), slot.dcount, reads, writes, pwrites)
        return ins

    def barrier(self):
        deps = {}
        for e in self.engs.values():
            if e.count > 0:
                deps[e.name] = e.count
        for b in self.dbufs:
            if b.dcount > 0 and b not in getattr(self, "async_bufs", ()):
                deps[("d", id(b))] = b.dcount
        for e in self.engs.values():
            d2 = {k: v for k, v in deps.items() if k != e.name}
            self._wait(e, d2)

    def finish(self, bufs, engname="sync"):
        eng = self.engs[engname]
        deps = {}
        for b in bufs:
            self._merge(deps, b.w)
        self._wait(eng, deps)

    def close(self):
        self.stack.close()


def scope_if(kb, cond):
    if cond:
        with kb.scope() as sc:
            yield sc


class Rot:
    def __init__(self, items):
        self.items = items
        self.i = 0

    def next(self):
        it = self.items[self.i % len(self.items)]
        self.i += 1
        return it


def emit_rstd(kb, xt, bx, st, bst, junk, bjunk, eps_ap=None, beps=None):
    kb.op("scalar", lambda e: e.activation(out=junk, in_=xt, func=AF.Square, accum_out=st[:, 0:1]),
          reads=[bx], writes=[bst] + ([bjunk] if bjunk is not None else []))
    kb.op("scalar", lambda e: e.activation(out=st[:, 1:2], in_=st[:, 0:1], func=AF.Ln, bias=eps_ap, scale=1.0 / D),
          reads=[bst, beps], writes=[bst])
    kb.op("scalar", lambda e: e.activation(out=st[:, 2:3], in_=st[:, 1:2], func=AF.Exp, scale=-0.5),
          reads=[bst], writes=[bst])


def emit_transpose8(kb, hb, bhb, pT, bpT, ident, bid, dst, bdst, dst_pw=False, nk=8, evac="scalar"):
    for k in range(nk):
        kb.op("tensor", lambda e, k=k: e.transpose(out=pT[:, k * 128:(k + 1) * 128], in_=hb[:, k * 128:(k + 1) * 128],
                                                   identity=ident),
              reads=[bhb, bid], writes=[bpT] if k == 0 else (), pwrites=[bpT] if k > 0 else (), track=(k == nk - 1))
    src = pT[:, 0:nk * 128].rearrange("p (k t) -> p k t", k=nk)
    kw = dict(pwrites=[bdst]) if dst_pw else dict(writes=[bdst])
    if evac == "scalar":
        kb.op("scalar", lambda e: e.copy(out=dst, in_=src), reads=[bpT], **kw)
    else:
        kb.op("vector", lambda e: e.tensor_copy(out=dst, in_=src), reads=[bpT], **kw)


def build_ffn(KIN, final, nc=None, kb=None, T=None):
    fused = T is not None
    if not fused:
        nc = bass.Bass("TRN2", target_bir_lowering=False)
    KC = KIN // 128
    NT = 16
    dt = (lambda n, s, d, k: T[n]) if fused else (lambda n, s, d, k: nc.dram_tensor(n, s, d, kind=k).ap())
    inT = dt("inT", [KIN, 2048], BF16, "ExternalInput")
    x = dt("x", [2048, D], F32, "ExternalInput")
    w_o = dt("w_o", [KIN, D], F32, "ExternalInput")
    w_in = dt("w_in", [D, 2 * DFF], F32, "ExternalInput")
    w_out = dt("w_out", [DFF, D], F32, "ExternalInput")
    g_ffn = dt("g_ffn", [128, D], F32, "ExternalInput")
    g_next = dt("g_next", [128, D], F32, "ExternalInput")
    identd = dt("ident", [128, 128], F32, "ExternalInput")
    if final:
        out = dt("out", [2048, D], F32, "ExternalOutput")
    else:
        x_out = dt("x_out", [2048, D], F32, "ExternalOutput")
        hT_out = dt("hT_out", [D, 2048], BF16, "ExternalOutput")
    x1s = dt("x1s", [2048, D], F32, "Internal")

    if not fused:
        kb = KB(nc)
    bouts = kb.bufs("o", 40)
    with kb.scope(barrier=fused) as P:
        epst = P.sb("epst", [128, 1], F32); beps = kb.buf("eps")
        kb.op("vector", lambda e: e.memset(epst[:], EPS), writes=[beps])
        gt_ffn = P.sb("gt_ffn", [128, D], F32); bgf = kb.buf("gf")
        gt_next = P.sb("gt_next", [128, D], F32); bgn = kb.buf("gn")
        ident = P.sb("identb", [128, 128], BF16); bid = kb.buf("id")
        st_t = [P.sb("st%d" % i, [128, 4], F32) for i in range(3)]
        st = Rot(list(zip(st_t, kb.bufs("st", 3))))
        kb.dma("sync", gt_ffn[:], g_ffn[:, :], bgf, writes=[bgf])
        kb.dma("sync", gt_next[:], g_next[:, :], bgn, writes=[bgn])
        kb.dma("gpsimd", ident[:], identd[:, :], bid, writes=[bid])
        pT_t = [P.ps("pT%d" % i, [128, 1024], BF16) for i in range(2)]
        pT = Rot(list(zip(pT_t, kb.bufs("pT", 2))))
        pf_t = [P.ps("pf%d" % i, [128, 512], F32) for i in range(6)]
        pf = Rot(list(zip(pf_t, kb.bufs("pf", 6))))
        baT = kb.bufs("aT", 4)
        bhT = kb.bufs("hT", NT)
        bx1 = kb.bufs("x1s", NT)
        H = Scope(kb)
        hT = H.sb("hT", [128, 8, 2048], BF16)

        with kb.scope() as S1:
            wo = S1.sb("wo", [128, KC, D], BF16); bwo = kb.buf("wo")
            w_o_v = w_o.rearrange("(kc p) n -> p kc n", p=128)
            for k0 in range(0, KC, 4):
                kb.dma("gpsimd", wo[:, k0:k0 + 4, :], w_o_v[:, k0:k0 + 4, :], bwo, pwrites=[bwo])
            inb_t = [S1.sb("inb%d" % i, [128, KC, 512], BF16) for i in range(2)]
            binb = kb.bufs("inb", 2)
            xt_t = [S1.sb("xt%d" % i, [128, D], F32) for i in range(4)]
            xts = Rot(list(zip(xt_t, kb.bufs("xt", 4))))
            hb_t = [S1.sb("hb%d" % i, [128, D], BF16) for i in range(3)]
            hbs = Rot(list(zip(hb_t, kb.bufs("hb", 3))))
            inT_v = None if (fused and "inT_dyn" in T) else inT.rearrange("(kc p) t -> p kc t", p=128)

            def load_s1(t):
                blk = t // 4
                if t % 4 == 0:
                    if fused and "inT_dyn" in T:
                        gsrc, jx, qname = T["inT_dyn"]
                        F_ = KC // 4
                        for r in range(4):
                            kb.dma(qname, inb_t[blk % 2][:, r * F_:(r + 1) * F_, :],
                                   gsrc[jx * 4 + blk, r].rearrange("(fh p) t -> p fh t", p=128), binb[blk % 2],
                                   reads=T.get("in_deps", []),
                                   writes=[binb[blk % 2]] if r == 0 else (), pwrites=[binb[blk % 2]] if r > 0 else ())
                    else:
                        kb.dma("sync", inb_t[blk % 2][:], inT_v[:, :, blk * 512:(blk + 1) * 512], binb[blk % 2],
                               writes=[binb[blk % 2]])
                xt, bxt = xts.next()
                kb.dma("sync", xt[:], x[t * 128:(t + 1) * 128, :], bxt, writes=[bxt])
                return xt, bxt

            pend = [load_s1(0), load_s1(1)]
            live = {}
            for i in range(NT + 2):
                if i < NT:
                    t = i
                    blk = t // 4
                    ib, bib = inb_t[blk % 2], binb[blk % 2]
                    xt, bxt = pend.pop(0)
                    if t + 2 < NT:
                        pend.append(load_s1(t + 2))
                    for c in range(2):
                        ps, bps = pf.next()
                        for kc in range(KC):
                            kb.op("tensor", lambda e, kc=kc, c=c, ps=ps, ib=ib: e.matmul(
                                out=ps[:], lhsT=ib[:, kc, (t % 4) * 128:(t % 4 + 1) * 128], rhs=wo[:, kc, c * 512:(c + 1) * 512],
                                start=(kc == 0), stop=(kc == KC - 1)),
                                reads=[bib, bwo], writes=[bps] if kc == 0 else (), pwrites=[bps] if kc > 0 else (),
                                track=(kc == KC - 1))
                        kb.op("vector", lambda e, c=c, ps=ps, xt=xt: e.tensor_tensor(
                            out=xt[:, c * 512:(c + 1) * 512], in0=xt[:, c * 512:(c + 1) * 512], in1=ps[:], op=ALU.add),
                            reads=[bps] + ([bxt] if c == 0 else []), writes=[bxt] if c == 0 else (),
                            pwrites=[bxt] if c == 1 else ())
                    kb.dma("sync", x1s[t * 128:(t + 1) * 128, :], xt[:], bxt, reads=[bxt], pwrites=[bx1[t]])
                    hb, bhb = hbs.next()
                    s_, bs_ = st.next()
                    emit_rstd(kb, xt[:], bxt, s_, bs_, hb[:], bhb, epst[:, 0:1], beps)
                    live[i] = (xt, bxt, hb, bhb, s_, bs_)
                j = i - 1
                if 0 <= j < NT:
                    xt, bxt, hb, bhb, s_, bs_ = live[j]
                    kb.op("vector", lambda e: e.scalar_tensor_tensor(
                        out=hb[:], in0=xt[:], scalar=s_[:, 2:3], in1=gt_ffn[:], op0=ALU.mult, op1=ALU.mult),
                        reads=[bxt, bs_, bgf], writes=[bhb])
                    p, bp = pT.next()
                    for k in range(8):
                        kb.op("tensor", lambda e, k=k: e.transpose(out=p[:, k * 128:(k + 1) * 128],
                                                                   in_=hb[:, k * 128:(k + 1) * 128], identity=ident[:]),
                              reads=[bhb, bid], writes=[bp] if k == 0 else (), pwrites=[bp] if k > 0 else (), track=(k == 7))
                    live[j] = (p, bp)
                j = i - 2
                if 0 <= j < NT:
                    p, bp = live.pop(j)
                    kb.op("scalar", lambda e: e.copy(out=hT[:, :, j * 128:(j + 1) * 128],
                                                     in_=p[:, 0:1024].rearrange("p (k t) -> p k t", k=8)),
                          reads=[bp], writes=[bhT[j]])

        aT = H.sb("aT", [128, 22, 2048], BF16)
        with kb.scope() as S2:
            slab_t = [S2.sb("slab%d" % i, [128, 8, 512], BF16) for i in range(3)]
            slabs = Rot(list(zip(slab_t, kb.bufs("slab", 3))))
            sg_t = [S2.sb("sg%d" % i, [128, 512], F32) for i in range(2)]
            sgs = Rot(list(zip(sg_t, kb.bufs("sg", 2))))
            w_in_v = w_in.rearrange("(kc p) n -> p kc n", p=128)
            NSL = DFF // 256

            def load_slab(f):
                sl, bsl = slabs.next()
                kb.dma("gpsimd", sl[:, :, 0:256], w_in_v[:, :, f * 256:(f + 1) * 256], bsl, writes=[bsl])
                kb.dma("gpsimd", sl[:, :, 256:512], w_in_v[:, :, DFF + f * 256:DFF + (f + 1) * 256], bsl, pwrites=[bsl])
                return sl, bsl

            pend = [load_slab(0), load_slab(1)]
            for f in range(NSL):
                sl, bsl = pend.pop(0)
                if f + 2 < NSL:
                    pend.append(load_slab(f + 2))
                for tb in range(4):
                    rb = bhT[tb * 4:(tb + 1) * 4]
                    for j in range(2):
                        ffc = f * 2 + j
                        pg, bpg = pf.next()
                        pu, bpu = pf.next()
                        for (pp, bpp, c0) in ((pg, bpg, j * 128), (pu, bpu, 256 + j * 128)):
                            for kc in range(8):
                                kb.op("tensor", lambda e, kc=kc, pp=pp, c0=c0, sl=sl, tb=tb: e.matmul(
                                    out=pp[:], lhsT=sl[:, kc, c0:c0 + 128], rhs=hT[:, kc, tb * 512:(tb + 1) * 512],
                                    start=(kc == 0), stop=(kc == 7)),
                                    reads=[bsl] + rb, writes=[bpp] if kc == 0 else (), pwrites=[bpp] if kc > 0 else (),
                                    track=(kc == 7))
                        sg, bsg = sgs.next()
                        kb.op("scalar", lambda e, sg=sg, pg=pg: e.activation(out=sg[:], in_=pg[:], func=AF.Silu),
                              reads=[bpg], writes=[bsg])
                        kb.op("vector", lambda e, sg=sg, pu=pu, ffc=ffc, tb=tb: e.tensor_tensor(
                            out=aT[:, ffc, tb * 512:(tb + 1) * 512], in0=sg[:], in1=pu[:], op=ALU.mult),
                            reads=[bsg, bpu], pwrites=[baT[tb]])

        with kb.scope() as S3:
            wout = S3.sb("wout", [128, 22, D], BF16); bwout = kb.buf("wout")
            w_out_v = w_out.rearrange("(kc p) n -> p kc n", p=128)
            for k0 in range(0, 22, 2):
                kb.dma("gpsimd", wout[:, k0:k0 + 2, :], w_out_v[:, k0:k0 + 2, :], bwout, pwrites=[bwout])
            xt_t = [S3.sb("xu%d" % i, [128, D], F32) for i in range(4)]
            xts = Rot(list(zip(xt_t, kb.bufs("xu", 4))))
            hb_t = [S3.sb("hc%d" % i, [128, D], BF16) for i in range(2)]
            hbs = Rot(list(zip(hb_t, kb.bufs("hc", 2))))
            if final:
                ot_t = [S3.sb("ot%d" % i, [128, D], F32) for i in range(2)]
                ots = Rot(list(zip(ot_t, kb.bufs("ot", 2))))
            else:
                stg_t = [S3.sb("stg%d" % i, [128, 8, 512], BF16) for i in range(1)] * 2
                bstg = kb.bufs("stg", 1) * 2
                hT_out_v = None if fused else hT_out.rearrange("(kc p) t -> p kc t", p=128)

            def load_x1(t):
                xt, bxt = xts.next()
                kb.dma("sync", xt[:], x1s[t * 128:(t + 1) * 128, :], bxt, reads=[bx1[t]], writes=[bxt])
                return xt, bxt

            def down_proj(t, xt, bxt):
                tb = t // 4
                for c in range(2):
                    ps, bps = pf.next()
                    for kc in range(22):
                        kb.op("tensor", lambda e, kc=kc, c=c, ps=ps: e.matmul(
                            out=ps[:], lhsT=aT[:, kc, t * 128:(t + 1) * 128], rhs=wout[:, kc, c * 512:(c + 1) * 512],
                            start=(kc == 0), stop=(kc == 21)),
                            reads=[baT[tb], bwout], writes=[bps] if kc == 0 else (), pwrites=[bps] if kc > 0 else (),
                            track=(kc == 21))
                    kb.op("vector", lambda e, c=c, ps=ps: e.tensor_tensor(
                        out=xt[:, c * 512:(c + 1) * 512], in0=xt[:, c * 512:(c + 1) * 512], in1=ps[:], op=ALU.add),
                        reads=[bps] + ([bxt] if c == 0 else []), writes=[bxt] if c == 0 else (),
                        pwrites=[bxt] if c == 1 else ())
                hb, bhb = hbs.next()
                s_, bs_ = st.next()
                emit_rstd(kb, xt[:], bxt, s_, bs_, hb[:], bhb, epst[:, 0:1], beps)
                return hb, bhb, s_, bs_

            def epilogue(t, xt, bxt, hb, bhb, s_, bs_):
                tb = t // 4
                if final:
                    ot, bot = ots.next()
                    kb.op("vector", lambda e: e.scalar_tensor_tensor(
                        out=ot[:], in0=xt[:], scalar=s_[:, 2:3], in1=gt_next[:], op0=ALU.mult, op1=ALU.mult),
                        reads=[bxt, bs_, bgn], writes=[bot])
                    kb.dma("sync", out[t * 128:(t + 1) * 128, :], ot[:], bot, reads=[bot], pwrites=[bouts[t]])
                else:
                    kb.dma("sync", x_out[t * 128:(t + 1) * 128, :], xt[:], bxt, reads=[bxt], pwrites=[bouts[t]])
                    kb.op("vector", lambda e: e.scalar_tensor_tensor(
                        out=hb[:], in0=xt[:], scalar=s_[:, 2:3], in1=gt_next[:], op0=ALU.mult, op1=ALU.mult),
                        reads=[bxt, bs_, bgn], writes=[bhb])
                    p, bp = pT.next()
                    sg_, bsg_ = stg_t[tb % 2], bstg[tb % 2]
                    emit_transpose8(kb, hb, bhb, p, bp, ident[:], bid, sg_[:, :, (t % 4) * 128:(t % 4 + 1) * 128], bsg_,
                                    dst_pw=(t % 4 != 0))
                    if t % 4 == 3:
                        if fused:
                            kb.dma("sync", T["hT_out4"][tb].rearrange("(kc p) t -> p kc t", p=128), sg_[:], bsg_,
                                   reads=[bsg_], pwrites=[bouts[20 + tb]])
                            T["after_blk"](tb, [bouts[20 + tb]])
                        else:
                            kb.dma("sync", hT_out_v[:, :, tb * 512:(tb + 1) * 512], sg_[:], bsg_, reads=[bsg_],
                                   pwrites=[bouts[20 + tb]])

            pend = [load_x1(0), load_x1(1)]
            prev = None
            for t in range(NT):
                xt, bxt = pend.pop(0)
                if t + 2 < NT:
                    pend.append(load_x1(t + 2))
                cur = (t, xt, bxt) + down_proj(t, xt, bxt)
                if prev is not None:
                    epilogue(*prev)
                prev = cur
            epilogue(*prev)
        kb.finish(bouts)
        H.stack.close()
    if fused:
        return bouts
    kb.close()
    return nc


_CACHE = {}
_TIMES = []


def _run(nc, in_maps):
    import os
    if os.environ.get("K_TRACE"):
        res = run_bass_kernel_spmd(nc, in_maps, core_ids=list(range(NCORES)), trace=True)
        _TIMES.append(res.exec_time_ns)
        print("exec_time_ns", res.exec_time_ns, flush=True)
        return res
    return run_bass_kernel_spmd(nc, in_maps, core_ids=list(range(NCORES)))


def _get(name, fn):
    if name not in _CACHE:
        _CACHE[name] = fn()
    return _CACHE[name]


def _bcast(v):
    return np.ascontiguousarray(np.broadcast_to(np.asarray(v, np.float32)[None, :], (128, v.shape[0])))


def run_ffn_phase(final, inT_list, x_list, w_o, w_in, w_out, g_ffn, g_next):
    KIN = w_o.shape[0]
    nc = _get(("ffn", KIN, final), lambda: build_ffn(KIN, final))
    ident = np.eye(128, dtype=np.float32)
    gf, gn = _bcast(g_ffn), _bcast(g_next)
    in_maps = []
    for c in range(NCORES):
        in_maps.append({"inT": inT_list[c], "x": x_list[c], "w_o": w_o, "w_in": w_in, "w_out": w_out,
                        "g_ffn": gf, "g_next": gn, "ident": ident})
    res = _run(nc, in_maps)
    return res.results


def build_fox(nc=None, kb=None, T=None):
    fused = T is not None
    TT = T
    if not fused:
        nc = bass.Bass("TRN2", target_bir_lowering=False)
    dt = (lambda n, s, d, k: T[n]) if fused else (lambda n, s, d, k: nc.dram_tensor(n, s, d, kind=k).ap())
    x = dt("x", [S, D], F32, "ExternalInput")
    wqk = dt("wqk", [D, 512], F32, "ExternalInput")
    wvf = dt("wvf", [D, 320], F32, "ExternalInput")
    nbfd = dt("nbf", [128, 2], F32, "ExternalInput")
    g_mix = dt("g_mix", [128, D], F32, "ExternalInput")
    identd = dt("ident", [128, 128], F32, "ExternalInput")
    trid = dt("tri", [128, 128], F32, "ExternalInput")
    onesd = dt("ones", [128, 128], F32, "ExternalInput")
    maskd = dt("maskneg", [128, 128], F32, "ExternalInput")
    oT = dt("oT", [256, S], BF16, "ExternalOutput")
    import os
    STOP = int(os.environ.get("FOX_STOP", "9"))
    SUB = int(os.environ.get("FOX_SUB", "9"))
    VB = int(os.environ.get("FOX_V", "3"))
    NT = S // 128
    NQB = S // 512
    scale = 128.0 ** -0.5

    if not fused:
        kb = KB(nc)
    bouts = kb.bufs("o", 2 * NQB)
    with kb.scope(barrier=fused) as P:
        epst = P.sb("epst", [128, 1], F32); beps = kb.buf("eps")
        kb.op("vector", lambda e: e.memset(epst[:], EPS), writes=[beps])
        gt = P.sb("gt", [128, D], F32); bgt = kb.buf("gt")
        ident = P.sb("identb", [128, 128], BF16); bid = kb.buf("id")
        maskn = P.sb("maskn", [128, 128], BF16); bmk = kb.buf("mk")
        tri = P.sb("tri_sb", [128, 128], F32); btri = kb.buf("tri")
        ones = P.sb("ones_sb", [128, 128], F32); bones = kb.buf("ones")
        nbf = P.sb("nbf_sb", [128, 2], F32); bnbf = kb.buf("nbf")
        kb.dma("sync", gt[:], g_mix[:, :], bgt, writes=[bgt])
        kb.dma("gpsimd", ident[:], identd[:, :], bid, writes=[bid])
        kb.dma("gpsimd", maskn[:], maskd[:, :], bmk, writes=[bmk])
        kb.dma("sync", tri[:], trid[:, :], btri, writes=[btri])
        kb.dma("sync", ones[:], onesd[:, :], bones, writes=[bones])
        kb.dma("sync", nbf[:], nbfd[:, :], bnbf, writes=[bnbf])
        st_t = [P.sb("st%d" % i, [128, 4], F32) for i in range(4)]
        st = Rot(list(zip(st_t, kb.bufs("st", 4))))
        QT = [P.sb("QT%d" % h, [128, S], BF16) for h in range(2)]
        KT = [P.sb("KT%d" % h, [128, S], BF16) for h in range(2)]
        bQT = [kb.bufs("QT%d_" % h, NQB) for h in range(2)]
        bKT = [kb.bufs("KT%d_" % h, NQB) for h in range(2)]
        VP = P.sb("VP", [128, NT, 2, 132], BF16); bVP = kb.bufs("VP", NT)
        Fl = P.sb("Fl", [128, NT, 2], F32); bF = kb.buf("F")
        CSP = P.sb("CSP", [128, NT, 2], F32); bCSP = kb.buf("CSP")
        PINC = P.sb("PINC", [128, NT, 2], F32); bPINC = kb.buf("PINC")
        bones_col = kb.buf("onescol")
        kb.op("vector", lambda e: e.memset(VP[:, :, :, 128:129], 1.0), writes=[bones_col])

        for S1 in scope_if(kb, STOP >= 1):
            wqk_sb = S1.sb("wqk_sb", [128, 8, 512], BF16); bwqk = kb.buf("wqk")
            wvf_sb = S1.sb("wvf_sb", [128, 8, 320], BF16); bwvf = kb.buf("wvf")
            kb.dma("gpsimd", wqk_sb[:], wqk.rearrange("(kc p) n -> p kc n", p=128), bwqk, writes=[bwqk])
            kb.dma("gpsimd", wvf_sb[:], wvf.rearrange("(kc p) n -> p kc n", p=128), bwvf, writes=[bwvf])
            xt_t = [S1.sb("xt%d" % i, [128, D], F32) for i in range(5)]
            xts = Rot(list(zip(xt_t, kb.bufs("xt", 5))))
            hb_t = [S1.sb("hb%d" % i, [128, D], BF16) for i in range(3)]
            hbs = Rot(list(zip(hb_t, kb.bufs("hb", 3))))
            hTb_t = [S1.sb("hTb%d" % i, [128, 8, 512], BF16) for i in range(2)]
            bhTb = [kb.bufs("hTb%d_" % i, 4) for i in range(2)]
            pT_t = [S1.ps("pT%d" % i, [128, 1024], BF16) for i in range(2)]
            pT = Rot(list(zip(pT_t, kb.bufs("pT", 2))))
            pf_t = [S1.ps("pf%d" % i, [128, 512], F32) for i in range(6)]
            pf = Rot(list(zip(pf_t, kb.bufs("pf", 6))))

            def load_x(t):
                xt, bxt = xts.next()
                kb.dma("sync", xt[:], x[t * 128:(t + 1) * 128, :], bxt, writes=[bxt])
                return xt, bxt

            def blk_matmuls(blk):
                hTb, bh = hTb_t[blk % 2], bhTb[blk % 2]
                for idx in range(4 if SUB >= 2 else 0):
                    ps, bps = pf.next()
                    for kc in range(8):
                        kb.op("tensor", lambda e, kc=kc, ps=ps, idx=idx: e.matmul(
                            out=ps[:], lhsT=wqk_sb[:, kc, idx * 128:(idx + 1) * 128], rhs=hTb[:, kc, :],
                            start=(kc == 0), stop=(kc == 7)),
                            reads=[bwqk] + bh, writes=[bps] if kc == 0 else (), pwrites=[bps] if kc > 0 else (),
                            track=(kc == 7))
                    dstT, bd = (QT, bQT) if idx < 2 else (KT, bKT)
                    h = idx % 2
                    kb.op("scalar", lambda e, ps=ps, dstT=dstT, h=h: e.copy(
                        out=dstT[h][:, blk * 512:(blk + 1) * 512], in_=ps[:]), reads=[bps], writes=[bd[h][blk]])
                for ti in range(4 if SUB >= 3 else 0):
                    t = blk * 4 + ti
                    ps, bps = pf.next()
                    for kc in range(8):
                        kb.op("tensor", lambda e, kc=kc, ps=ps, ti=ti: e.matmul(
                            out=ps[:, 0:320], lhsT=hTb[:, kc, ti * 128:(ti + 1) * 128], rhs=wvf_sb[:, kc, :],
                            start=(kc == 0), stop=(kc == 7)),
                            reads=[bwvf, bh[ti]], writes=[bps] if kc == 0 else (), pwrites=[bps] if kc > 0 else (),
                            track=(kc == 7))
                    kb.op("vector", lambda e, ps=ps, t=t: e.tensor_copy(
                        out=VP[:, t, :, 0:128], in_=ps[:, 0:256].rearrange("p (h d) -> p h d", h=2)),
                        reads=[bps], writes=[bVP[t]])
                    kb.op("vector", lambda e, ps=ps, t=t: e.tensor_copy(out=Fl[:, t, :], in_=ps[:, 256:258]),
                          reads=[bps], pwrites=[bF])

            pend = [load_x(0), load_x(1), load_x(2)]
            live = {}
            for i in range(NT + 2):
                if i < NT:
                    xt, bxt = pend.pop(0)
                    if i + 3 < NT:
                        pend.append(load_x(i + 3))
                    hb, bhb = hbs.next()
                    s_, bs_ = st.next()
                    emit_rstd(kb, xt[:], bxt, s_, bs_, hb[:], bhb, epst[:, 0:1], beps)
                    live[i] = (xt, bxt, hb, bhb, s_, bs_)
                j = i - 1
                if 0 <= j < NT:
                    xt, bxt, hb, bhb, s_, bs_ = live[j]
                    kb.op("vector", lambda e: e.scalar_tensor_tensor(
                        out=hb[:], in0=xt[:], scalar=s_[:, 2:3], in1=gt[:], op0=ALU.mult, op1=ALU.mult),
                        reads=[bxt, bs_, bgt], writes=[bhb])
                    p, bp = pT.next()
                    for k in range(8):
                        kb.op("tensor", lambda e, k=k: e.transpose(out=p[:, k * 128:(k + 1) * 128],
                                                                   in_=hb[:, k * 128:(k + 1) * 128], identity=ident[:]),
                              reads=[bhb, bid], writes=[bp] if k == 0 else (), pwrites=[bp] if k > 0 else (), track=(k == 7))
                    live[j] = (p, bp)
                j = i - 2
                if 0 <= j < NT:
                    p, bp = live.pop(j)
                    blk, ti = j // 4, j % 4
                    hTb, bh = hTb_t[blk % 2], bhTb[blk % 2]
                    kb.op("scalar", lambda e: e.copy(out=hTb[:, :, ti * 128:(ti + 1) * 128],
                                                     in_=p[:, 0:1024].rearrange("p (k t) -> p k t", k=8)),
                          reads=[bp], writes=[bh[ti]])
                    if ti == 3:
                        blk_matmuls(blk)

        bdbg = []
        for S2 in scope_if(kb, STOP >= 2):
            E = S2.sb("E", [128, NT, 2], F32); bE = kb.buf("E")
            SP = S2.sb("SP", [128, NT, 2], F32); bSP = kb.buf("SP")
            Tt = S2.sb("Tt", [128, NT, 2], F32); bTt = kb.buf("Tt")
            TM = S2.sb("TM", [128, NT, 2], F32); bTM = kb.buf("TM")
            onec = S2.sb("onec", [128, NT], F32); bonec = kb.buf("onec")
            psW = S2.ps("psW", [128, 128], F32); bpsW = kb.buf("psW")
            psT = S2.ps("psT", [128, 128], F32); bpsT = kb.buf("psT")
            kb.op("vector", lambda e: e.memset(onec[:], 1.0), writes=[bonec])
            for h in range(2):
                kb.op("scalar", lambda e, h=h: e.activation(out=E[:, :, h], in_=Fl[:, :, h], func=AF.Exp,
                                                            bias=nbf[:, h:h + 1], scale=-1.0),
                      reads=[bF, bnbf], pwrites=[bE])
            kb.op("scalar", lambda e: e.activation(out=SP[:], in_=E[:], func=AF.Ln, bias=onec[:, 0:1], scale=1.0),
                  reads=[bE, bonec], writes=[bSP])
            spf = SP[:].rearrange("p t h -> p (t h)")
            kb.op("tensor", lambda e: e.matmul(out=psW[:], lhsT=tri[:], rhs=spf, start=True, stop=True),
                  reads=[btri, bSP], writes=[bpsW])
            kb.op("tensor", lambda e: e.matmul(out=psT[:], lhsT=ones[:], rhs=spf, start=True, stop=True),
                  reads=[bones, bSP], writes=[bpsT])
            kb.op("vector", lambda e: e.tensor_copy(out=Tt[:].rearrange("p t h -> p (t h)"), in_=psT[:]),
                  reads=[bpsT], writes=[bTt])
            for h in range(2):
                kb.op("vector", lambda e, h=h: e.tensor_tensor_scan(out=PINC[:, :, h], data0=onec[:], data1=Tt[:, :, h],
                                                                    initial=0.0, op0=ALU.mult, op1=ALU.add),
                      reads=[bonec, bTt], pwrites=[bPINC])
            kb.op("vector", lambda e: e.tensor_tensor(out=TM[:], in0=PINC[:], in1=Tt[:], op=ALU.subtract),
                  reads=[bPINC, bTt], writes=[bTM])
            kb.op("vector", lambda e: e.tensor_tensor(out=CSP[:].rearrange("p t h -> p (t h)"), in0=psW[:],
                                                      in1=TM[:].rearrange("p t h -> p (t h)"), op=ALU.add),
                  reads=[bpsW, bTM], writes=[bCSP])

        for S3 in scope_if(kb, STOP >= 3):
            pS_t = [S3.ps("pS%d" % i, [128, 2, 512], F32) for i in range(2)]
            pS = Rot(list(zip(pS_t, kb.bufs("pS", 2))))
            pO_t = [S3.ps("pO%d" % i, [128, 512], F32) for i in range(2)]
            bpO = kb.bufs("pO", 2)
            pR = S3.ps("pR", [128, 512], F32); bpR = kb.buf("pR")
            onesb = S3.sb("onesb", [128, 128], BF16); bonesb = kb.buf("onesb")
            kb.op("vector", lambda e: e.memset(onesb[:], 1.0), writes=[bonesb])
            PT_t = [S3.sb("PT%d" % i, [128, 2, 512], BF16) for i in range(3)]
            PTs = Rot(list(zip(PT_t, kb.bufs("PT", 3))))
            bq_t = [S3.sb("bq%d" % i, [128, NT], F32) for i in range(2)]
            bbqs = kb.bufs("bq", 2)
            Pend = S3.sb("Pend", [128, NT, 2], F32); bPend = kb.buf("Pend")
            wexp = S3.sb("wexp", [128, NT, 2], F32); bwexp = kb.buf("wexp")
            WB = S3.sb("WB", [128, NT, 2, 128], BF16); bWB = kb.buf("WB")
            pinc_pairs = PINC[:].rearrange("p (a two) h -> p a two h", two=2)
            pend_pairs = Pend[:].rearrange("p (a two) h -> p a two h", two=2)
            for two in range(2):
                kb.op("vector", lambda e, two=two: e.tensor_copy(out=pend_pairs[:, :, two, :], in_=pinc_pairs[:, :, 1, :]),
                      reads=[bPINC], writes=[bPend] if two == 0 else (), pwrites=[bPend] if two == 1 else ())
            kb.op("vector", lambda e: e.tensor_tensor(out=wexp[:], in0=CSP[:], in1=Pend[:], op=ALU.subtract),
                  reads=[bCSP, bPend], writes=[bwexp])
            kb.op("scalar", lambda e: e.activation(out=wexp[:], in_=wexp[:], func=AF.Exp), reads=[bwexp], writes=[bwexp])
            for kt in range(NT):
                for h in range(2):
                    kb.op("vector", lambda e, kt=kt, h=h: e.tensor_scalar_mul(
                        out=VP[:, kt, h, 0:128], in0=VP[:, kt, h, 0:128], scalar1=wexp[:, kt, h:h + 1]),
                        reads=[bwexp, bVP[kt]], writes=[bVP[kt]])
                    kb.op("vector", lambda e, kt=kt, h=h: e.tensor_scalar_mul(
                        out=WB[:, kt, h, :], in0=onesb[:], scalar1=wexp[:, kt, h:h + 1]),
                        reads=[bwexp, bonesb], pwrites=[bWB])
            ssA_t = [S3.sb("ssA%d" % i, [128, 512], F32) for i in range(2)]; bssA = kb.bufs("ssA", 2)
            rs_t = [S3.sb("rs%d" % i, [128, 512], F32) for i in range(2)]
            rss = Rot(list(zip(rs_t, kb.bufs("rs", 2))))
            os_t = [S3.sb("os%d" % i, [128, 512], BF16) for i in range(2)]
            oss = Rot(list(zip(os_t, kb.bufs("os", 2))))
            blocks = [(h, qb) for h in range(2) for qb in range(NQB)]

            def emit_bias(i):
                h, qb = blocks[i]
                nk = 4 * qb + 4
                bq, bbq = bq_t[i % 2], bbqs[i % 2]
                kb.op("vector", lambda e: e.tensor_scalar(
                    out=bq[:, 0:nk], in0=Pend[:, 0:nk, h], scalar1=PINC[:, nk - 3, h:h + 1], scalar2=None,
                    op0=ALU.subtract), reads=[bPend, bPINC], writes=[bbq])

            units = []
            for i, (h, qb) in enumerate(blocks):
                for k0 in range(0, 4 * qb, 2):
                    units.append((i, [k0, k0 + 1]))
                for kt in range(4 * qb, 4 * qb + 4):
                    units.append((i, [kt]))

            def emit_qk(u):
                i, kts = units[u]
                h, qb = blocks[i]
                ps, bps = pS.next()
                for ui, kt in enumerate(kts):
                    j0 = max(0, kt - 4 * qb)
                    c0 = j0 * 128
                    diag = kt >= 4 * qb
                    kb.op("tensor", lambda e: e.matmul(
                        out=ps[:, ui, c0:512], lhsT=KT[h][:, kt * 128:(kt + 1) * 128],
                        rhs=QT[h][:, qb * 512 + c0:(qb + 1) * 512], start=True, stop=(not diag)),
                        reads=[bKT[h][kt // 4], bQT[h][qb]], writes=[bps] if ui == 0 else (),
                        pwrites=[bps] if ui > 0 else (), track=(not diag and ui == len(kts) - 1))
                    if diag:
                        kb.op("tensor", lambda e: e.matmul(
                            out=ps[:, ui, c0:c0 + 128], lhsT=ident[:], rhs=maskn[:], start=False, stop=True),
                            reads=[bid, bmk], pwrites=[bps])
                return ps, bps

            emit_bias(0)
            nxt = emit_qk(0)
            for u, (i, kts) in enumerate(units):
                h, qb = blocks[i]
                nk = 4 * qb + 4
                ps, bps = nxt
                if kts[0] == 0 and i + 1 < len(blocks):
                    emit_bias(i + 1)
                if u + 1 < len(units):
                    nxt = emit_qk(u + 1)
                bq, bbq = bq_t[i % 2], bbqs[i % 2]
                pO, bO = pO_t[i % 2], bpO[i % 2]
                ssA, bsA = ssA_t[i % 2], bssA[i % 2]
                pt, bpt = PTs.next()
                if len(kts) == 2:
                    kb.op("scalar", lambda e: e.activation(
                        out=pt[:].rearrange("p a b -> p (a b)"), in_=ps[:].rearrange("p a b -> p (a b)"),
                        func=AF.Exp, bias=bq[:, kts[0]:kts[0] + 1], scale=scale), reads=[bps, bbq], writes=[bpt])
                else:
                    c0 = max(0, kts[0] - 4 * qb) * 128
                    kb.op("scalar", lambda e: e.activation(
                        out=pt[:, 0, c0:512], in_=ps[:, 0, c0:512], func=AF.Exp, bias=bq[:, kts[0]:kts[0] + 1], scale=scale),
                        reads=[bps, bbq], writes=[bpt])
                for ui, kt in enumerate(kts):
                    c0 = max(0, kt - 4 * qb) * 128
                    kb.op("tensor", lambda e: e.matmul(
                        out=pO[:, c0:512], lhsT=VP[:, kt, h, 0:128], rhs=pt[:, ui, c0:512], start=(kt == 0), stop=(kt == nk - 1),
                        skip_group_check=True),
                        reads=[bpt, bVP[kt]], writes=[bO] if kt == 0 else (), pwrites=[bO] if kt > 0 else ())
                    if kt == 0:
                        kb.op("vector", lambda e: e.tensor_scalar_mul(out=ssA[:], in0=pt[:, ui, :], scalar1=wexp[:, 0, h:h + 1]),
                              reads=[bpt, bwexp], writes=[bsA])
                    elif kt % 2 == 1:
                        first_pe = (kt == 1)
                        kb.op("tensor", lambda e: e.matmul(out=pR[:, c0:512], lhsT=WB[:, kt, h, :], rhs=pt[:, ui, c0:512],
                                                           start=first_pe, stop=False, skip_group_check=True),
                              reads=[bpt, bWB], writes=[bpR] if first_pe else (), pwrites=() if first_pe else [bpR])
                    else:
                        kb.op("vector", lambda e: e.scalar_tensor_tensor(
                            out=ssA[:, c0:512], in0=pt[:, ui, c0:512], scalar=wexp[:, kt, h:h + 1], in1=ssA[:, c0:512],
                            op0=ALU.mult, op1=ALU.add), reads=[bpt, bwexp, bsA], writes=[bsA])
                    if kt == nk - 1:
                        kb.op("tensor", lambda e: e.matmul(out=pR[:], lhsT=ones[:], rhs=ssA[:], start=False, stop=True,
                                                           skip_group_check=True),
                              reads=[bones, bsA], pwrites=[bpR])
                        rs, brs = rss.next()
                        kb.op("vector", lambda e: e.reciprocal(out=rs[:], in_=pR[:]), reads=[bpR], writes=[brs])
                        osb, bos = oss.next()
                        kb.op("vector", lambda e: e.tensor_tensor(out=osb[:], in0=pO[:], in1=rs[:], op=ALU.mult),
                              reads=[bO, brs], writes=[bos])
                        odst = (T["oT4"][qb, h * 128:(h + 1) * 128, :] if fused
                                else oT[h * 128:(h + 1) * 128, qb * 512:(qb + 1) * 512])
                        kb.dma("sync", odst, osb[:], bos, reads=[bos], pwrites=[bouts[i]])
                        if fused and h == 1 and "after_shard" in TT:
                            TT["after_shard"](qb, [bouts[qb], bouts[NQB + qb]])
        kb.finish(bouts + bdbg)
    if fused:
        return bouts
    kb.close()
    return nc


def run_fox_phase(x, fox_w_in, fox_b_f, g_mix):
    nc = _get("fox", build_fox)
    ident = np.eye(128, dtype=np.float32)
    tri = np.triu(np.ones((128, 128), np.float32))
    ones = np.ones((128, 128), np.float32)
    kk = np.arange(128)
    maskneg = np.where(kk[:, None] > kk[None, :], NEG, 0.0).astype(np.float32)
    gm = _bcast(g_mix)
    in_maps = []
    for c in range(NCORES):
        b, hp = c // 4, c % 4
        h0, h1 = 2 * hp, 2 * hp + 1
        cols_qk = np.concatenate([np.arange(h0 * 128, h0 * 128 + 128), np.arange(h1 * 128, h1 * 128 + 128),
                                  D + np.arange(h0 * 128, h0 * 128 + 128), D + np.arange(h1 * 128, h1 * 128 + 128)])
        cols_vf = np.concatenate([2 * D + np.arange(h0 * 128, h0 * 128 + 128), 2 * D + np.arange(h1 * 128, h1 * 128 + 128),
                                  np.array([3 * D + h0, 3 * D + h1])])
        wvf_p = np.zeros((D, 320), np.float32)
        wvf_p[:, :258] = fox_w_in[:, cols_vf]
        nbf = np.ascontiguousarray(np.broadcast_to(-fox_b_f[[h0, h1]][None, :], (128, 2))).astype(np.float32)
        in_maps.append({"x": np.ascontiguousarray(x[b]), "wqk": np.ascontiguousarray(fox_w_in[:, cols_qk]),
                        "wvf": wvf_p, "nbf": nbf, "g_mix": gm,
                        "ident": ident, "tri": tri, "ones": ones, "maskneg": maskneg})
    res = _run(nc, in_maps)
    return res.results


def build_ret(nc=None, kb=None, T=None):
    fused = T is not None
    TT = T
    if not fused:
        nc = bass.Bass("TRN2", target_bir_lowering=False)
    dt = (lambda n, s, d, k: T[n]) if fused else (lambda n, s, d, k: nc.dram_tensor(n, s, d, kind=k).ap())
    h1T = dt("h1T", [4, D, 2048], BF16, "ExternalInput")
    wqk = dt("wqk", [D, 512], F32, "ExternalInput")
    wv = dt("wv", [D, 512], F32, "ExternalInput")
    wg = dt("wg", [D, 512], F32, "ExternalInput")
    cosd = dt("cosT", [128, S], F32, "ExternalInput")
    sind = dt("sinT", [128, S], F32, "ExternalInput")
    qdecd = dt("qdec", [128, 512], F32, "ExternalInput")
    cstd = dt("cst", [128, 2], F32, "ExternalInput")
    dmaskd = dt("dmask", [128, 64], F32, "ExternalInput")
    identd = dt("ident", [128, 128], F32, "ExternalInput")
    ygT = dt("ygT", [512, S], BF16, "ExternalOutput")
    NBLK = S // 512

    if not fused:
        kb = KB(nc)
    bouts = kb.bufs("o", NBLK)
    with kb.scope(barrier=fused) as P:
        ident = P.sb("identb", [128, 128], BF16); bid = kb.buf("id")
        epst = P.sb("epst", [128, 1], F32); beps = kb.buf("eps")
        kb.op("vector", lambda e: e.memset(epst[:], EPS), writes=[beps])
        qdec = P.sb("qdec_sb", [128, 512], F32); bqdec = kb.buf("qdec")
        cst = P.sb("cst_sb", [128, 2], F32); bcst = kb.buf("cst")
        dmask = P.sb("dmask_sb", [128, 64], F32); bdm = kb.buf("dm")
        wqk_sb = P.sb("wqk_sb", [128, 8, 512], BF16); bwqk = kb.buf("wqk")
        wv_sb = P.sb("wv_sb", [128, 8, 512], BF16); bwv = kb.buf("wv")
        wg_sb = P.sb("wg_sb", [128, 8, 512], BF16); bwg = kb.buf("wg")
        kb.dma("gpsimd", ident[:], identd[:, :], bid, writes=[bid])
        kb.dma("sync", qdec[:], qdecd[:, :], bqdec, writes=[bqdec])
        kb.dma("sync", cst[:], cstd[:, :], bcst, writes=[bcst])
        kb.dma("sync", dmask[:], dmaskd[:, :], bdm, writes=[bdm])
        for (wsb, wd, bw) in ((wqk_sb, wqk, bwqk), (wv_sb, wv, bwv), (wg_sb, wg, bwg)):
            kb.dma("gpsimd", wsb[:], wd.rearrange("(kc p) n -> p kc n", p=128), bw, writes=[bw])
        Sf = P.sb("Sf", [128, 2, 512], F32); bSf = kb.buf("Sf")
        Sb_t = [P.sb("Sb%d" % i, [128, 2, 512], BF16) for i in range(2)]
        bSb = kb.bufs("Sb", 2)
        kb.op("vector", lambda e: e.memset(Sf[:], 0.0), writes=[bSf])
        SF_INIT_DONE = True
        kb.op("vector", lambda e: e.memset(Sb_t[0][:], 0.0), writes=[bSb[0]])
        hbl_t = [P.sb("hbl%d" % i, [128, 8, 512], BF16) for i in range(2)]
        bhbl = kb.bufs("hbl", 2)
        cs_t = [P.sb("cs%d" % i, [128, 2, 512], F32) for i in range(2)]
        bcs = kb.bufs("cs", 2)
        ksb = P.sb("ksb", [128, 2, 512], F32); bksb = kb.buf("ksb")
        tq_t = [P.sb("tq%d" % i, [128, 512], F32) for i in range(2)]; btq = kb.bufs("tq", 2)
        tk_t = [P.sb("tk%d" % i, [128, 512], F32) for i in range(2)]; btk = kb.bufs("tk", 2)
        QR_t = [P.sb("QR%d" % i, [128, 2, 512], BF16) for i in range(2)]; bQR = kb.bufs("QR", 2)
        QD_t = [P.sb("QD%d" % i, [128, 2, 512], BF16) for i in range(2)]; bQD = kb.bufs("QD", 2)
        KR_t = [P.sb("KR%d" % i, [128, 2, 512], BF16) for i in range(2)]; bKR = kb.bufs("KR", 2)
        KD_t = [P.sb("KD%d" % i, [128, 256], BF16) for i in range(3)]
        KDs = Rot(list(zip(KD_t, kb.bufs("KD", 3))))
        V_t = [P.sb("V%d" % i, [128, 512], BF16) for i in range(3)]
        Vs = Rot(list(zip(V_t, kb.bufs("V", 3))))
        SG_t = [P.sb("SG%d" % i, [128, 512], F32) for i in range(3)]
        SGs = Rot(list(zip(SG_t, kb.bufs("SG", 3))))
        aT_t = [P.sb("aTs%d" % i, [128, 64], BF16) for i in range(2)]
        aTs = Rot(list(zip(aT_t, kb.bufs("aTs", 2))))
        bn_t = [P.sb("bn%d" % i, [128, 12], F32) for i in range(2)]
        bns = Rot(list(zip(bn_t, kb.bufs("bn", 2))))
        yn_t = [P.sb("yn%d" % i, [128, 512], F32) for i in range(2)]
        yns = Rot(list(zip(yn_t, kb.bufs("yn", 2))))
        yg_t = [P.sb("yg%d" % i, [128, 512], BF16) for i in range(2)]
        ygs = Rot(list(zip(yg_t, kb.bufs("yg", 2))))
        stg_t = [P.sb("stg%d" % i, [128, 4, 512], BF16) for i in range(2)]
        bstg = kb.bufs("stg", 2)
        pB = P.ps("pBt", [128, 1024], BF16); bpB = kb.bufs("pB", 2)
        pA = P.ps("pAt", [128, 512], F32); bpA = kb.bufs("pA", 2)
        pY_t = [P.ps("pY%d" % i, [128, 512], F32) for i in range(2)]; bpY = kb.bufs("pY", 2)
        pF_t = [P.ps("pF%d" % i, [128, 512], F32) for i in range(4)]
        pF = Rot(list(zip(pF_t, kb.bufs("pF", 4))))
        qs = P.sb("qs", [128, 2, 512], F32); bqs = kb.bufs("qs", 2)
        SG4_t = [P.sb("SG4_%d" % i, [128, 512], F32) for i in range(4)]
        SG4 = list(zip(SG4_t, kb.bufs("SG4", 4)))
        V4_t = [P.sb("V4_%d" % i, [128, 512], BF16) for i in range(4)]
        V4 = list(zip(V4_t, kb.bufs("V4", 4)))
        KD4_t = [P.sb("KD4_%d" % i, [128, 256], BF16) for i in range(4)]
        KD4 = list(zip(KD4_t, kb.bufs("KD4", 4)))
        aT4_t = [P.sb("aT4_%d" % i, [128, 64], BF16) for i in range(4)]
        aT4 = list(zip(aT4_t, kb.bufs("aT4", 4)))

        def mm_group(ps, bps, pairs, reads):
            n = len(pairs)
            for i, (l, r) in enumerate(pairs):
                kb.op("tensor", lambda e, l=l, r=r, i=i: e.matmul(out=ps, lhsT=l, rhs=r, start=(i == 0), stop=(i == n - 1)),
                      reads=reads, writes=[bps] if i == 0 else (), pwrites=[bps] if i > 0 else (), track=(i == n - 1))

        def load_blk(blk):
            r, off = blk // 4, (blk % 4) * 512
            hb, bh = hbl_t[blk % 2], bhbl[blk % 2]
            if fused:
                kb.dma("sync", hb[:], h1T[blk % 4, r].rearrange("(kc p) t -> p kc t", p=128), bh,
                       reads=TT.get("in_deps", []), writes=[bh])
            else:
                kb.dma("sync", hb[:], h1T[r].rearrange("(kc p) t -> p kc t", p=128)[:, :, off:off + 512], bh, writes=[bh])
            cs, bc = cs_t[blk % 2], bcs[blk % 2]
            kb.dma("sync", cs[:, 0, :], cosd[:, blk * 512:(blk + 1) * 512], bc, writes=[bc])
            kb.dma("sync", cs[:, 1, :], sind[:, blk * 512:(blk + 1) * 512], bc, pwrites=[bc])

        def blk_prep(blk):
            hb, bh = hbl_t[blk % 2], bhbl[blk % 2]
            cs, bc = cs_t[blk % 2], bcs[blk % 2]
            QR, bqr = QR_t[blk % 2], bQR[blk % 2]
            QD, bqd = QD_t[blk % 2], bQD[blk % 2]
            KR, bkr = KR_t[blk % 2], bKR[blk % 2]
            cos, sin = cs[:, 0, :], cs[:, 1, :]
            for idx in range(4):
                ps, bps = pF.next()
                mm_group(ps[:], bps, [(wqk_sb[:, kc, idx * 128:(idx + 1) * 128], hb[:, kc, :]) for kc in range(8)],
                         [bwqk, bh])
                dst, bd = (qs, bqs) if idx < 2 else (ksb, None)
                if idx < 2:
                    kb.op("scalar", lambda e, ps=ps, idx=idx: e.copy(out=qs[:, idx, :], in_=ps[:]), reads=[bps], writes=[bqs[idx]])
                else:
                    kb.op("scalar", lambda e, ps=ps, idx=idx: e.copy(out=ksb[:, idx - 2, :], in_=ps[:]), reads=[bps],
                          writes=[bksb] if idx == 2 else (), pwrites=[bksb] if idx == 3 else ())
            tA, btA, tB, btB = tq_t[0], btq[0], tq_t[1], btq[1]
            V_ = "vector"
            kb.op(V_, lambda e: e.tensor_tensor(out=tA[:], in0=qs[:, 0, :], in1=cos, op=ALU.mult), reads=[bqs[0], bc], writes=[btA])
            kb.op(V_, lambda e: e.tensor_tensor(out=tB[:], in0=qs[:, 1, :], in1=sin, op=ALU.mult), reads=[bqs[1], bc], writes=[btB])
            kb.op(V_, lambda e: e.tensor_tensor(out=QR[:, 0, :], in0=tA[:], in1=tB[:], op=ALU.subtract),
                  reads=[btA, btB], writes=[bqr])
            kb.op(V_, lambda e: e.tensor_tensor(out=tA[:], in0=qs[:, 0, :], in1=sin, op=ALU.mult), reads=[bqs[0], bc], writes=[btA])
            kb.op(V_, lambda e: e.tensor_tensor(out=tB[:], in0=qs[:, 1, :], in1=cos, op=ALU.mult), reads=[bqs[1], bc], writes=[btB])
            kb.op(V_, lambda e: e.tensor_tensor(out=QR[:, 1, :], in0=tA[:], in1=tB[:], op=ALU.add),
                  reads=[btA, btB], pwrites=[bqr])
            for dc in range(2):
                kb.op(V_, lambda e, dc=dc: e.tensor_tensor(out=QD[:, dc, :], in0=QR[:, dc, :], in1=qdec[:], op=ALU.mult),
                      reads=[bqr, bqdec], writes=[bqd] if dc == 0 else (), pwrites=[bqd] if dc == 1 else ())
            tA, btA, tB, btB = tk_t[0], btk[0], tk_t[1], btk[1]
            G_ = "vector"
            kb.op(G_, lambda e: e.tensor_tensor(out=tA[:], in0=ksb[:, 0, :], in1=cos, op=ALU.mult), reads=[bksb, bc], writes=[btA])
            kb.op(G_, lambda e: e.tensor_tensor(out=tB[:], in0=ksb[:, 1, :], in1=sin, op=ALU.mult), reads=[bksb, bc], writes=[btB])
            kb.op(G_, lambda e: e.tensor_tensor(out=KR[:, 0, :], in0=tA[:], in1=tB[:], op=ALU.subtract),
                  reads=[btA, btB], writes=[bkr])
            kb.op(G_, lambda e: e.tensor_tensor(out=tA[:], in0=ksb[:, 0, :], in1=sin, op=ALU.mult), reads=[bksb, bc], writes=[btA])
            kb.op(G_, lambda e: e.tensor_tensor(out=tB[:], in0=ksb[:, 1, :], in1=cos, op=ALU.mult), reads=[bksb, bc], writes=[btB])
            kb.op(G_, lambda e: e.tensor_tensor(out=KR[:, 1, :], in0=tA[:], in1=tB[:], op=ALU.add),
                  reads=[btA, btB], pwrites=[bkr])

        def tile_prep(T):
            blk, ti = T // 4, T % 4
            tc0 = ti * 128
            hb, bh = hbl_t[blk % 2], bhbl[blk % 2]
            QR, bqr = QR_t[blk % 2], bQR[blk % 2]
            KR, bkr = KR_t[blk % 2], bKR[blk % 2]
            V, bV = V4[T % 4]
            SG, bSG = SG4[T % 4]
            KD, bKD = KD4[T % 4]
            aTb, baT = aT4[T % 4]
            pv, bpv = pF.next()
            mm_group(pv[:], bpv, [(hb[:, kc, tc0:tc0 + 128], wv_sb[:, kc, :]) for kc in range(8)], [bwv, bh])
            kb.op("scalar", lambda e: e.copy(out=V[:], in_=pv[:]), reads=[bpv], writes=[bV])
            pg, bpg = pF.next()
            mm_group(pg[:], bpg, [(hb[:, kc, tc0:tc0 + 128], wg_sb[:, kc, :]) for kc in range(8)], [bwg, bh])
            kb.op("scalar", lambda e: e.activation(out=SG[:], in_=pg[:], func=AF.Silu), reads=[bpg], writes=[bSG])
            for dc in range(2):
                kb.op("tensor", lambda e, dc=dc: e.transpose(
                    out=pB[:, dc * 128:(dc + 1) * 128], in_=KR[:, dc, tc0:tc0 + 128], identity=ident[:]),
                    reads=[bkr, bid], writes=[bpB[0]] if dc == 0 else (), pwrites=[bpB[0]] if dc == 1 else (), track=(dc == 1))
            kb.op("scalar", lambda e: e.activation(out=KD[:], in_=pB[:, 0:256], func=AF.Copy, scale=cst[:, 0:1]),
                  reads=[bpB[0], bcst], writes=[bKD])
            a0 = (T % 2) * 64
            for c in range(2):
                t0 = tc0 + c * 64
                for dc in range(2):
                    kb.op("tensor", lambda e, c=c, dc=dc, t0=t0: e.matmul(
                        out=pA[c * 64:(c + 1) * 64, a0:a0 + 64], lhsT=KR[:, dc, t0:t0 + 64], rhs=QR[:, dc, t0:t0 + 64],
                        start=(dc == 0), stop=(dc == 1), skip_group_check=True),
                        reads=[bkr, bqr], writes=[bpA[T % 2]] if (c == 0 and dc == 0) else (),
                        pwrites=() if (c == 0 and dc == 0) else [bpA[T % 2]], track=(c == 1 and dc == 1))
            kb.op("vector", lambda e: e.tensor_tensor(out=aTb[:], in0=pA[:, a0:a0 + 64], in1=dmask[:], op=ALU.mult),
                  reads=[bpA[T % 2], bdm], writes=[baT])

        state = {"cur": 0}

        def chunk(T, c, part, sidx=None):
            blk, ti = T // 4, T % 4
            tc0 = ti * 128
            lo, hi = c * 64, (c + 1) * 64
            t0 = tc0 + c * 64
            QD, bqd = QD_t[blk % 2], bQD[blk % 2]
            V, bV = V4[T % 4]
            KD, bKD = KD4[T % 4]
            aTb, baT = aT4[T % 4]
            py, bpy = pY_t[T % 2], bpY[T % 2]
            if part == 1:
                Sb, bsb = Sb_t[sidx], bSb[sidx]
                kb.op("tensor", lambda e: e.matmul(
                    out=py[lo:hi, :], lhsT=aTb[lo:hi, 0:64], rhs=V[lo:hi, :], start=True, stop=False),
                    reads=[baT, bV], writes=[bpy] if c == 0 else (), pwrites=[bpy] if c == 1 else (), track=False)
                for dc in range(2):
                    kb.op("tensor", lambda e, dc=dc: e.matmul(
                        out=py[lo:hi, :], lhsT=QD[:, dc, t0:t0 + 64], rhs=Sb[:, dc, :], start=False, stop=(dc == 1)),
                        reads=[bqd, bsb], pwrites=[bpy], track=(dc == 1))
                return
            pU = [pF.next(), pF.next()]
            for dc in range(2):
                pu, bpu = pU[dc]
                kb.op("tensor", lambda e, dc=dc, pu=pu: e.matmul(
                    out=pu[:], lhsT=KD[lo:hi, dc * 128:(dc + 1) * 128], rhs=V[lo:hi, :], start=True, stop=True),
                    reads=[bKD, bV], writes=[bpu])
            return pU

        def state_update(pU):
            nxt = 1 - state["cur"]
            for dc in range(2):
                pu, bpu = pU[dc]
                kb.op("vector", lambda e, dc=dc, pu=pu: e.scalar_tensor_tensor(
                    out=Sf[:, dc, :], in0=Sf[:, dc, :], scalar=cst[:, 1:2], in1=pu[:], op0=ALU.mult, op1=ALU.add),
                    reads=[bpu, bcst, bSfd[dc]], writes=[bSfd[dc]])
                kb.op("scalar", lambda e, dc=dc: e.copy(out=Sb_t[nxt][:, dc, :], in_=Sf[:, dc, :]), reads=[bSfd[dc]],
                      writes=[bSb[nxt]] if dc == 0 else (), pwrites=[bSb[nxt]] if dc == 1 else ())
            state["cur"] = nxt

        def post_a(T):
            py, bpy = pY_t[T % 2], bpY[T % 2]
            bn, bbn = bn_t[T % 2], bbns[T % 2]
            kb.op("vector", lambda e: e.bn_stats(out=bn[:, 0:6], in_=py[:]), reads=[bpy], writes=[bbn])
            kb.op("vector", lambda e: e.bn_aggr(out=bn[:, 6:8], in_=bn[:, 0:6]), reads=[bbn], writes=[bbn])
            kb.op("scalar", lambda e: e.activation(out=bn[:, 8:9], in_=bn[:, 7:8], func=AF.Ln, bias=epst[:, 0:1], scale=1.0),
                  reads=[bbn, beps], writes=[bbn])
            kb.op("scalar", lambda e: e.activation(out=bn[:, 9:10], in_=bn[:, 8:9], func=AF.Exp, scale=-0.5),
                  reads=[bbn], writes=[bbn])

        def post_b(T):
            blk, ti = T // 4, T % 4
            tc0 = ti * 128
            py, bpy = pY_t[T % 2], bpY[T % 2]
            bn, bbn = bn_t[T % 2], bbns[T % 2]
            SG, bSG = SG4[T % 4]
            stg, bsg = stg_t[blk % 2], bstg[blk % 2]
            yn, byn = yns.next()
            kb.op("vector", lambda e: e.tensor_scalar(
                out=yn[:], in0=py[:], scalar1=bn[:, 6:7], scalar2=bn[:, 9:10], op0=ALU.subtract, op1=ALU.mult),
                reads=[bpy, bbn], writes=[byn])
            yg, byg = ygs.next()
            kb.op("vector", lambda e: e.tensor_tensor(out=yg[:], in0=yn[:], in1=SG[:], op=ALU.mult),
                  reads=[byn, bSG], writes=[byg])
            for fc in range(4):
                kb.op("tensor", lambda e, fc=fc: e.transpose(
                    out=pB[:, 512 + fc * 128:512 + (fc + 1) * 128], in_=yg[:, fc * 128:(fc + 1) * 128], identity=ident[:]),
                    reads=[byg, bid], writes=[bpB[1]] if fc == 0 else (), pwrites=[bpB[1]] if fc > 0 else (), track=(fc == 3))
            kb.op("scalar", lambda e: e.copy(
                out=stg[:, :, tc0:tc0 + 128], in_=pB[:, 512:1024].rearrange("p (f t) -> p f t", f=4)),
                reads=[bpB[1]], writes=[bsg] if ti == 0 else (), pwrites=[bsg] if ti > 0 else ())
            if ti == 3:
                ydst = (TT["ygT4"][blk].rearrange("(fc p) t -> p fc t", p=128)
                        if fused else ygT.rearrange("(fc p) t -> p fc t", p=128)[:, :, blk * 512:(blk + 1) * 512])
                kb.dma("sync", ydst, stg[:], bsg, reads=[bsg], pwrites=[bouts[blk]])
                if fused and "after_shard" in TT:
                    TT["after_shard"](blk, [bouts[blk]])

        bbns = kb.bufs("bnb", 2)
        byns = kb.bufs("ynb", 2)
        bSfd = kb.bufs("Sfd", 2)
        NTT = S // 128
        load_blk(0)
        load_blk(1)
        blk_prep(0)
        tile_prep(0)
        for T in range(NTT):
            blk, ti = T // 4, T % 4
            uA = chunk(T, 0, 0)
            uB = chunk(T, 1, 0)
            chunk(T, 0, 1, state["cur"])
            state_update(uA)
            idxA = state["cur"]
            state_update(uB)
            if T + 1 < NTT:
                if (T + 1) % 4 == 0:
                    nb = (T + 1) // 4
                    if nb + 1 < NBLK:
                        load_blk(nb + 1)
                if (T + 2) % 4 == 0 and (T + 2) // 4 < NBLK:
                    blk_prep((T + 2) // 4)
                tile_prep(T + 1)
            chunk(T, 1, 1, idxA)
            post_a(T)
            if T >= 1:
                post_b(T - 1)
        post_b(NTT - 1)
        kb.finish(bouts)
    if fused:
        return bouts
    kb.close()
    return nc


def _ret_consts():
    half = 128
    inv = (10000.0 ** (-np.arange(half, dtype=np.float32) / half)).astype(np.float32)
    ang = (np.arange(S, dtype=np.float32)[:, None] * inv[None, :]).astype(np.float32)
    cosT = np.ascontiguousarray(np.cos(ang).T.astype(np.float32))
    sinT = np.ascontiguousarray(np.sin(ang).T.astype(np.float32))
    return cosT, sinT


def run_ret_phase(h1T_full, ret_w_in):
    nc = _get("ret", build_ret)
    ident = np.eye(128, dtype=np.float32)
    cosT, sinT = _get("retc", _ret_consts)
    idx = np.arange(64, dtype=np.float32)
    in_maps = []
    for c in range(NCORES):
        b, hd = c // 4, c % 4
        lg = np.log(np.float32(1.0 - 2.0 ** (-5.0 - hd))).astype(np.float32)
        qd = np.exp(lg * (idx + 1.0)).astype(np.float32)
        kd = np.exp(lg * (63.0 - idx)).astype(np.float32) * np.float32(256.0 ** -0.5)
        cd = np.exp(lg * 64.0).astype(np.float32)
        dm = (np.exp(lg * np.abs(idx[:, None] - idx[None, :])) * np.float32(256.0 ** -0.5)).astype(np.float32)
        qdec = np.ascontiguousarray(np.broadcast_to(np.tile(qd, 8)[None, :], (128, 512))).astype(np.float32)
        cst = np.stack([np.tile(kd, 2), np.full(128, cd, np.float32)], axis=1).astype(np.float32)
        dmask = np.concatenate([dm, dm], axis=0).astype(np.float32)
        wqk = np.concatenate([ret_w_in[:, hd * 256:(hd + 1) * 256], ret_w_in[:, D + hd * 256:D + (hd + 1) * 256]], axis=1)
        wv = ret_w_in[:, 2 * D + hd * 512:2 * D + (hd + 1) * 512]
        wg = ret_w_in[:, 4 * D + hd * 512:4 * D + (hd + 1) * 512]
        in_maps.append({"h1T": h1T_full[b], "wqk": np.ascontiguousarray(wqk), "wv": np.ascontiguousarray(wv),
                        "wg": np.ascontiguousarray(wg), "cosT": cosT, "sinT": sinT, "qdec": qdec, "cst": cst,
                        "dmask": dmask, "ident": ident})
    res = _run(nc, in_maps)
    return res.results


def build_fused():
    nc = bass.Bass("TRN2", target_bir_lowering=False)
    EI = lambda n, s, d=F32: nc.dram_tensor(n, s, d, kind="ExternalInput").ap()
    IN = lambda n, s, d: nc.dram_tensor(n, s, d, kind="Internal").ap()
    kb = KB(nc)
    rg = [[0, 1, 2, 3], [4, 5, 6, 7]]
    j = nc.partition_id() % 4
    ident = EI("ident", [128, 128])
    oT4 = IN("oT4", [16, 256, 512], BF16)
    oT_g = IN("oT_g", [16, 4, 256, 512], BF16)
    TA = {"x": EI("x", [S, D]), "wqk": EI("a_wqk", [D, 512]), "wvf": EI("a_wvf", [D, 320]), "nbf": EI("a_nbf", [128, 2]),
          "g_mix": EI("a_gmix", [128, D]), "ident": ident, "tri": EI("tri", [128, 128]), "ones": EI("ones", [128, 128]),
          "maskneg": EI("maskneg", [128, 128]), "oT": None, "oT4": oT4}
    kb.pfx = "A_"
    bcA, bcB, bcC = kb.buf("cA"), kb.buf("cB"), kb.buf("cC")
    for b_ in (bcA, bcB, bcC):
        kb.mkdma(b_)
    kb.async_bufs = [bcA, bcB, bcC]

    def shard_done_A(a, deps):
        kb.coll("AllGather", rg, oT4[a], oT_g[a].rearrange("r f t -> (r f) t"), bcA, reads=deps, pwrites=[bcA])

    TA["after_shard"] = shard_done_A
    build_fox(nc, kb, TA)
    x2s = IN("x2s", [2048, D], F32)
    h1T_loc = IN("h1T_loc", [4, D, 512], BF16)
    h1T_g = IN("h1T_g", [4, 4, D, 512], BF16)
    TB = {"inT": None, "inT_dyn": (oT_g, j, "sync"), "in_deps": [bcA], "x": EI("xs", [2048, D]),
          "w_o": EI("b_wo", [D, D]), "w_in": EI("b_win", [D, 2 * DFF]), "w_out": EI("b_wout", [DFF, D]),
          "g_ffn": EI("b_gffn", [128, D]), "g_next": EI("b_gnext", [128, D]), "ident": ident,
          "x_out": x2s, "hT_out": None, "hT_out4": h1T_loc, "x1s": IN("x1s_b", [2048, D], F32)}
    kb.pfx = "B_"

    def blk_done_B(tb, deps):
        kb.coll("AllGather", rg, h1T_loc[tb], h1T_g[tb].rearrange("r f t -> (r f) t"), bcB, reads=deps, pwrites=[bcB])

    TB["after_blk"] = blk_done_B
    build_ffn(D, False, nc, kb, TB)
    ygT4 = IN("ygT4", [16, 512, 512], BF16)
    ygT_g = IN("ygT_g", [16, 4, 512, 512], BF16)
    TC = {"in_deps": [bcB], "h1T": h1T_g, "wqk": EI("c_wqk", [D, 512]), "wv": EI("c_wv", [D, 512]), "wg": EI("c_wg", [D, 512]),
          "cosT": EI("cosT", [128, S]), "sinT": EI("sinT", [128, S]), "qdec": EI("qdec", [128, 512]),
          "cst": EI("cst", [128, 2]), "dmask": EI("dmask", [128, 64]), "ident": ident, "ygT": None, "ygT4": ygT4}
    kb.pfx = "C_"

    def shard_done_C(a, deps):
        kb.coll("AllGather", rg, ygT4[a], ygT_g[a].rearrange("r f t -> (r f) t"), bcC, reads=deps, pwrites=[bcC])

    TC["after_shard"] = shard_done_C
    build_ret(nc, kb, TC)
    TD = {"inT": None, "inT_dyn": (ygT_g, j, "scalar"), "in_deps": [bcC], "x": x2s,
          "w_o": EI("d_wo", [2 * D, D]), "w_in": EI("d_win", [D, 2 * DFF]), "w_out": EI("d_wout", [DFF, D]),
          "g_ffn": EI("d_gffn", [128, D]), "g_next": EI("d_gnext", [128, D]), "ident": ident,
          "out": nc.dram_tensor("out", [2048, D], F32, kind="ExternalOutput").ap(), "x1s": IN("x1s_d", [2048, D], F32)}
    kb.pfx = "D_"
    build_ffn(2 * D, True, nc, kb, TD)
    kb.close()
    return nc


def _fox_core_inputs(c, fox_w_in, fox_b_f):
    hp = c % 4
    h0, h1 = 2 * hp, 2 * hp + 1
    cols_qk = np.concatenate([np.arange(h0 * 128, h0 * 128 + 128), np.arange(h1 * 128, h1 * 128 + 128),
                              D + np.arange(h0 * 128, h0 * 128 + 128), D + np.arange(h1 * 128, h1 * 128 + 128)])
    cols_vf = np.concatenate([2 * D + np.arange(h0 * 128, h0 * 128 + 128), 2 * D + np.arange(h1 * 128, h1 * 128 + 128),
                              np.array([3 * D + h0, 3 * D + h1])])
    wvf_p = np.zeros((D, 320), np.float32)
    wvf_p[:, :258] = fox_w_in[:, cols_vf]
    nbf = np.ascontiguousarray(np.broadcast_to(-fox_b_f[[h0, h1]][None, :], (128, 2))).astype(np.float32)
    return np.ascontiguousarray(fox_w_in[:, cols_qk]), wvf_p, nbf


def _ret_core_inputs(c, ret_w_in):
    hd = c % 4
    idx = np.arange(64, dtype=np.float32)
    lg = np.log(np.float32(1.0 - 2.0 ** (-5.0 - hd))).astype(np.float32)
    qd = np.exp(lg * (idx + 1.0)).astype(np.float32)
    kd = np.exp(lg * (63.0 - idx)).astype(np.float32) * np.float32(256.0 ** -0.5)
    cd = np.exp(lg * 64.0).astype(np.float32)
    dm = (np.exp(lg * np.abs(idx[:, None] - idx[None, :])) * np.float32(256.0 ** -0.5)).astype(np.float32)
    qdec = np.ascontiguousarray(np.broadcast_to(np.tile(qd, 8)[None, :], (128, 512))).astype(np.float32)
    cst = np.stack([np.tile(kd, 2), np.full(128, cd, np.float32)], axis=1).astype(np.float32)
    dmask = np.concatenate([dm, dm], axis=0).astype(np.float32)
    wqk = np.concatenate([ret_w_in[:, hd * 256:(hd + 1) * 256], ret_w_in[:, D + hd * 256:D + (hd + 1) * 256]], axis=1)
    wv = ret_w_in[:, 2 * D + hd * 512:2 * D + (hd + 1) * 512]
    wg = ret_w_in[:, 4 * D + hd * 512:4 * D + (hd + 1) * 512]
    return (np.ascontiguousarray(wqk), np.ascontiguousarray(wv), np.ascontiguousarray(wg), qdec, cst, dmask)


def kernel_fused(x, norm_mix, norm_ffn, fox_w_in, fox_b_f, fox_w_out, ret_w_in, ret_w_out, ffn_w_in, ffn_w_out, final_norm):
    nc = _get("fused", build_fused)
    ident = np.eye(128, dtype=np.float32)
    tri = np.triu(np.ones((128, 128), np.float32))
    ones = np.ones((128, 128), np.float32)
    kk = np.arange(128)
    maskneg = np.where(kk[:, None] > kk[None, :], NEG, 0.0).astype(np.float32)
    cosT, sinT = _get("retc", _ret_consts)
    in_maps = []
    for c in range(NCORES):
        b, jj = c // 4, c % 4
        a_wqk, a_wvf, a_nbf = _fox_core_inputs(c, fox_w_in[0], fox_b_f[0])
        c_wqk, c_wv, c_wg, qdec, cst, dmask = _ret_core_inputs(c, ret_w_in[0])
        in_maps.append({
            "ident": ident, "tri": tri, "ones": ones, "maskneg": maskneg,
            "x": np.ascontiguousarray(x[b]), "xs": np.ascontiguousarray(x[b, jj * 2048:(jj + 1) * 2048, :]),
            "a_wqk": a_wqk, "a_wvf": a_wvf, "a_nbf": a_nbf, "a_gmix": _bcast(norm_mix[0]),
            "b_wo": fox_w_out[0], "b_win": ffn_w_in[0], "b_wout": ffn_w_out[0],
            "b_gffn": _bcast(norm_ffn[0]), "b_gnext": _bcast(norm_mix[1]),
            "c_wqk": c_wqk, "c_wv": c_wv, "c_wg": c_wg, "cosT": cosT, "sinT": sinT, "qdec": qdec, "cst": cst, "dmask": dmask,
            "d_wo": ret_w_out[0], "d_win": ffn_w_in[1], "d_wout": ffn_w_out[1],
            "d_gffn": _bcast(norm_ffn[1]), "d_gnext": _bcast(final_norm),
        })
    res = _run(nc, in_maps).results
    out = np.empty((NB, S, D), np.float32)
    for c in range(NCORES):
        out[c // 4, (c % 4) * 2048:(c % 4 + 1) * 2048, :] = res[c]["out"]
    return out


FUSED = True


def kernel(x, norm_mix, norm_ffn, fox_w_in, fox_b_f, fox_w_out, ret_w_in, ret_w_out, ffn_w_in, ffn_w_out, final_norm):
    x = np.asarray(x, np.float32)
    f32 = lambda a: np.asarray(a, np.float32)
    norm_mix, norm_ffn, final_norm = f32(norm_mix), f32(norm_ffn), f32(final_norm)
    fox_w_in, fox_b_f, fox_w_out = f32(fox_w_in), f32(fox_b_f), f32(fox_w_out)
    ret_w_in, ret_w_out, ffn_w_in, ffn_w_out = f32(ret_w_in), f32(ret_w_out), f32(ffn_w_in), f32(ffn_w_out)
    if FUSED:
        return kernel_fused(x, norm_mix, norm_ffn, fox_w_in, fox_b_f, fox_w_out, ret_w_in, ret_w_out, ffn_w_in,
                            ffn_w_out, final_norm)
    resA = run_fox_phase(x, fox_w_in[0], fox_b_f[0], norm_mix[0])
    oT_full = [np.concatenate([resA[b * 4 + j]["oT"] for j in range(4)], axis=0) for b in range(NB)]
    inT = [np.ascontiguousarray(oT_full[c // 4][:, (c % 4) * 2048:(c % 4 + 1) * 2048]) for c in range(NCORES)]
    xs = [np.ascontiguousarray(x[c // 4, (c % 4) * 2048:(c % 4 + 1) * 2048, :]) for c in range(NCORES)]
    resB = run_ffn_phase(False, inT, xs, fox_w_out[0], ffn_w_in[0], ffn_w_out[0], norm_ffn[0], norm_mix[1])
    h1T_full = [np.stack([resB[b * 4 + j]["hT_out"] for j in range(4)], axis=0) for b in range(NB)]
    x2 = [resB[c]["x_out"] for c in range(NCORES)]
    resC = run_ret_phase(h1T_full, ret_w_in[0])
    ygT_full = [np.concatenate([resC[b * 4 + j]["ygT"] for j in range(4)], axis=0) for b in range(NB)]
    inT = [np.ascontiguousarray(ygT_full[c // 4][:, (c % 4) * 2048:(c % 4 + 1) * 2048]) for c in range(NCORES)]
    resD = run_ffn_phase(True, inT, x2, ret_w_out[0], ffn_w_in[1], ffn_w_out[1], norm_ffn[1], final_norm)
    out = np.empty((NB, S, D), np.float32)
    for c in range(NCORES):
        out[c // 4, (c % 4) * 2048:(c % 4 + 1) * 2048, :] = resD[c]["out"]
    return out
```

```python
import numpy as np
from contextlib import ExitStack, contextmanager
import ml_dtypes
import concourse.bass as bass
import concourse.mybir as mybir
from concourse.bass_utils import run_bass_kernel_spmd

F32 = mybir.dt.float32
BF16 = mybir.dt.bfloat16
AF = mybir.ActivationFunctionType
ALU = mybir.AluOpType

D = 1024
S = 8192
NB = 2
DFF = 2816
EPS = 1e-6
NCORES = 8
NEG = -30000.0

SAME_ENG_SYNC = True


class Eng:
    def __init__(self, name, h, sem):
        self.name = name
        self.h = h
        self.sem = sem
        self.count = 0
        self.seen = {}


class Buf:
    __slots__ = ("name", "w", "r", "dsem", "dcount")

    def __init__(self, name):
        self.name = name
        self.w = {}
        self.r = {}
        self.dsem = None
        self.dcount = 0


class Scope:
    def __init__(self, kb):
        self.kb = kb
        self.stack = ExitStack()

    def sb(self, name, shape, dtype):
        return self.stack.enter_context(self.kb.nc.sbuf_tensor(self.kb.pfx + name, shape, dtype))

    def ps(self, name, shape, dtype):
        return self.stack.enter_context(self.kb.nc.psum_tensor(self.kb.pfx + name, shape, dtype))


class KB:
    def __init__(self, nc):
        self.nc = nc
        self.stack = ExitStack()
        self.engs = {}
        self.semof = {}
        self.dbufs = []
        for name in ["tensor", "vector", "scalar", "gpsimd", "sync"]:
            sem = self.stack.enter_context(nc.semaphore("pg_" + name))
            e = Eng(name, getattr(nc, name), sem)
            self.engs[name] = e
            self.semof[name] = sem
        self.nsem = 0
        self.pfx = ""

    def buf(self, name):
        return Buf(name)

    def bufs(self, name, n):
        return [Buf("%s%d" % (name, i)) for i in range(n)]

    def mkdma(self, b):
        if b.dsem is None:
            b.dsem = self.stack.enter_context(self.nc.semaphore("d%d" % self.nsem))
            self.nsem += 1
            self.semof[("d", id(b))] = b.dsem
            self.dbufs.append(b)

    @contextmanager
    def scope(self, barrier=True):
        sc = Scope(self)
        try:
            yield sc
        finally:
            if barrier:
                self.barrier()
            sc.stack.close()

    def _wait(self, eng, deps):
        for key, val in deps.items():
            if key == eng.name:
                if eng.name in ("tensor", "sync") or not SAME_ENG_SYNC:
                    continue
            if eng.seen.get(key, 0) >= val:
                continue
            eng.h.wait_ge(self.semof[key], val)
            eng.seen[key] = val

    @staticmethod
    def _merge(d, s):
        for k, v in s.items():
            if d.get(k, 0) < v:
                d[k] = v

    def _deps(self, reads, writes, pwrites):
        deps = {}
        for b in reads:
            self._merge(deps, b.w)
        for b in writes:
            self._merge(deps, b.w)
            self._merge(deps, b.r)
        for b in pwrites:
            self._merge(deps, b.r)
        return deps

    def _update(self, key, val, reads, writes, pwrites):
        for b in reads:
            if b.r.get(key, 0) < val:
                b.r[key] = val
        for b in writes:
            b.w = {key: val}
            b.r = {}
        for b in pwrites:
            if b.w.get(key, 0) < val:
                b.w[key] = val

    def op(self, engname, fn, reads=(), writes=(), pwrites=(), track=True):
        eng = self.engs[engname]
        self._wait(eng, self._deps(reads, writes, pwrites))
        ins = fn(eng.h)
        if track:
            eng.count += 1
            ins.then_inc(eng.sem, 1)
            val = eng.count
        else:
            val = eng.count + 1
        self._update(eng.name, val, reads, writes, pwrites)
        return ins

    def dma(self, qname, out, in_, slot, reads=(), writes=(), pwrites=(), **kw):
        q = self.engs[qname]
        self.mkdma(slot)
        self._wait(q, self._deps(reads, writes, pwrites))
        ins = q.h.dma_start(out=out, in_=in_, **kw)
        slot.dcount += 16
        ins.then_inc(slot.dsem, 16)
        self._update(("d", id(slot)), slot.dcount, reads, writes, pwrites)
        return ins

    def coll(self, kind, groups, src, dst, slot, reads=(), writes=(), pwrites=()):
        q = self.engs["gpsimd"]
        self.mkdma(slot)
        self._wait(q, self._deps(reads, writes, pwrites))
        ins = q.h.collective_compute(kind, ALU.bypass, replica_groups=groups, ins=[src], outs=[dst])
        slot.dcount += 1
        ins.then_inc(slot.dsem)
        self._update(("d", id(slot)), slot.dcount, reads, writes, pwrites)
        return ins

    def barrier(self):
        deps = {}
        for e in self.engs.values():
            if e.count > 0:
                deps[e.name] = e.count
        for b in self.dbufs:
            if b.dcount > 0 and b not in getattr(self, "async_bufs", ()):
                deps[("d", id(b))] = b.dcount
        for e in self.engs.values():
            d2 = {k: v for k, v in deps.items() if k != e.name}
            self._wait(e, d2)

    def finish(self, bufs, engname="sync"):
        eng = self.engs[engname]
        deps = {}
        for b in bufs:
            self._merge(deps, b.w)
        self._wait(eng, deps)

    def close(self):
        self.stack.close()


def scope_if(kb, cond):
    if cond:
        with kb.scope() as sc:
            yield sc


class Rot:
    def __init__(self, items):
        self.items = items
        self.i = 0

    def next(self):
        it = self.items[self.i % len(self.items)]
        self.i += 1
        return it


def emit_rstd(kb, xt, bx, st, bst, junk, bjunk, eps_ap=None, beps=None):
    kb.op("scalar", lambda e: e.activation(out=junk, in_=xt, func=AF.Square, accum_out=st[:, 0:1]),
          reads=[bx], writes=[bst] + ([bjunk] if bjunk is not None else []))
    kb.op("scalar", lambda e: e.activation(out=st[:, 1:2], in_=st[:, 0:1], func=AF.Ln, bias=eps_ap, scale=1.0 / D),
          reads=[bst, beps], writes=[bst])
    kb.op("scalar", lambda e: e.activation(out=st[:, 2:3], in_=st[:, 1:2], func=AF.Exp, scale=-0.5),
          reads=[bst], writes=[bst])


def emit_transpose8(kb, hb, bhb, pT, bpT, ident, bid, dst, bdst, dst_pw=False, nk=8, evac="scalar"):
    for k in range(nk):
        kb.op("tensor", lambda e, k=k: e.transpose(out=pT[:, k * 128:(k + 1) * 128], in_=hb[:, k * 128:(k + 1) * 128],
                                                   identity=ident),
              reads=[bhb, bid], writes=[bpT] if k == 0 else (), pwrites=[bpT] if k > 0 else (), track=(k == nk - 1))
    src = pT[:, 0:nk * 128].rearrange("p (k t) -> p k t", k=nk)
    kw = dict(pwrites=[bdst]) if dst_pw else dict(writes=[bdst])
    if evac == "scalar":
        kb.op("scalar", lambda e: e.copy(out=dst, in_=src), reads=[bpT], **kw)
    else:
        kb.op("vector", lambda e: e.tensor_copy(out=dst, in_=src), reads=[bpT], **kw)


def build_ffn(KIN, final, nc=None, kb=None, T=None):
    fused = T is not None
    if not fused:
        nc = bass.Bass("TRN2", target_bir_lowering=False)
    KC = KIN // 128
    NT = 16
    dt = (lambda n, s, d, k: T[n]) if fused else (lambda n, s, d, k: nc.dram_tensor(n, s, d, kind=k).ap())
    inT = dt("inT", [KIN, 2048], BF16, "ExternalInput")
    x = dt("x", [2048, D], F32, "ExternalInput")
    w_o = dt("w_o", [KIN, D], F32, "ExternalInput")
    w_in = dt("w_in", [D, 2 * DFF], F32, "ExternalInput")
    w_out = dt("w_out", [DFF, D], F32, "ExternalInput")
    g_ffn = dt("g_ffn", [128, D], F32, "ExternalInput")
    g_next = dt("g_next", [128, D], F32, "ExternalInput")
    identd = dt("ident", [128, 128], F32, "ExternalInput")
    if final:
        out = dt("out", [2048, D], F32, "ExternalOutput")
    else:
        x_out = dt("x_out", [2048, D], F32, "ExternalOutput")
        hT_out = dt("hT_out", [D, 2048], BF16, "ExternalOutput")
    x1s = dt("x1s", [2048, D], F32, "Internal")

    if not fused:
        kb = KB(nc)
    bouts = kb.bufs("o", 40)
    with kb.scope(barrier=fused) as P:
        epst = P.sb("epst", [128, 1], F32); beps = kb.buf("eps")
        kb.op("vector", lambda e: e.memset(epst[:], EPS), writes=[beps])
        gt_ffn = P.sb("gt_ffn", [128, D], F32); bgf = kb.buf("gf")
        gt_next = P.sb("gt_next", [128, D], F32); bgn = kb.buf("gn")
        ident = P.sb("identb", [128, 128], BF16); bid = kb.buf("id")
        st_t = [P.sb("st%d" % i, [128, 4], F32) for i in range(3)]
        st = Rot(list(zip(st_t, kb.bufs("st", 3))))
        kb.dma("sync", gt_ffn[:], g_ffn[:, :], bgf, writes=[bgf])
        kb.dma("sync", gt_next[:], g_next[:, :], bgn, writes=[bgn])
        kb.dma("gpsimd", ident[:], identd[:, :], bid, writes=[bid])
        pT_t = [P.ps("pT%d" % i, [128, 1024], BF16) for i in range(2)]
        pT = Rot(list(zip(pT_t, kb.bufs("pT", 2))))
        pf_t = [P.ps("pf%d" % i, [128, 512], F32) for i in range(6)]
        pf = Rot(list(zip(pf_t, kb.bufs("pf", 6))))
        baT = kb.bufs("aT", 4)
        bhT = kb.bufs("hT", NT)
        bx1 = kb.bufs("x1s", NT)
        H = Scope(kb)
        hT = H.sb("hT", [128, 8, 2048], BF16)

        with kb.scope() as S1:
            wo = S1.sb("wo", [128, KC, D], BF16); bwo = kb.buf("wo")
            w_o_v = w_o.rearrange("(kc p) n -> p kc n", p=128)
            for k0 in range(0, KC, 4):
                kb.dma("gpsimd", wo[:, k0:k0 + 4, :], w_o_v[:, k0:k0 + 4, :], bwo, pwrites=[bwo])
            inb_t = [S1.sb("inb%d" % i, [128, KC, 512], BF16) for i in range(2)]
            binb = kb.bufs("inb", 2)
            xt_t = [S1.sb("xt%d" % i, [128, D], F32) for i in range(4)]
            xts = Rot(list(zip(xt_t, kb.bufs("xt", 4))))
            hb_t = [S1.sb("hb%d" % i, [128, D], BF16) for i in range(3)]
            hbs = Rot(list(zip(hb_t, kb.bufs("hb", 3))))
            inT_v = None if (fused and "inT_dyn" in T) else inT.rearrange("(kc p) t -> p kc t", p=128)

            def load_s1(t):
                blk = t // 4
                if t % 4 == 0:
                    if fused and "inT_dyn" in T:
                        gsrc, jx, qname = T["inT_dyn"]
                        F_ = KC // 4
                        for r in range(4):
                            kb.dma(qname, inb_t[blk % 2][:, r * F_:(r + 1) * F_, :],
                                   gsrc[jx * 4 + blk, r].rearrange("(fh p) t -> p fh t", p=128), binb[blk % 2],
                                   reads=T.get("in_deps", []),
                                   writes=[binb[blk % 2]] if r == 0 else (), pwrites=[binb[blk % 2]] if r > 0 else ())
                    else:
                        kb.dma("sync", inb_t[blk % 2][:], inT_v[:, :, blk * 512:(blk + 1) * 512], binb[blk % 2],
                               writes=[binb[blk % 2]])
                xt, bxt = xts.next()
                kb.dma("sync", xt[:], x[t * 128:(t + 1) * 128, :], bxt, writes=[bxt])
                return xt, bxt

            pend = [load_s1(0), load_s1(1)]
            live = {}
            for i in range(NT + 2):
                if i < NT:
                    t = i
                    blk = t // 4
                    ib, bib = inb_t[blk % 2], binb[blk % 2]
                    xt, bxt = pend.pop(0)
                    if t + 2 < NT:
                        pend.append(load_s1(t + 2))
                    for c in range(2):
                        ps, bps = pf.next()
                        for kc in range(KC):
                            kb.op("tensor", lambda e, kc=kc, c=c, ps=ps, ib=ib: e.matmul(
                                out=ps[:], lhsT=ib[:, kc, (t % 4) * 128:(t % 4 + 1) * 128], rhs=wo[:, kc, c * 512:(c + 1) * 512],
                                start=(kc == 0), stop=(kc == KC - 1)),
                                reads=[bib, bwo], writes=[bps] if kc == 0 else (), pwrites=[bps] if kc > 0 else (),
                                track=(kc == KC - 1))
                        kb.op("vector", lambda e, c=c, ps=ps, xt=xt: e.tensor_tensor(
                            out=xt[:, c * 512:(c + 1) * 512], in0=xt[:, c * 512:(c + 1) * 512], in1=ps[:], op=ALU.add),
                            reads=[bps] + ([bxt] if c == 0 else []), writes=[bxt] if c == 0 else (),
                            pwrites=[bxt] if c == 1 else ())
                    kb.dma("gpsimd", x1s[t * 128:(t + 1) * 128, :], xt[:], bxt, reads=[bxt], pwrites=[bx1[t]])
                    hb, bhb = hbs.next()
                    s_, bs_ = st.next()
                    emit_rstd(kb, xt[:], bxt, s_, bs_, hb[:], bhb, epst[:, 0:1], beps)
                    live[i] = (xt, bxt, hb, bhb, s_, bs_)
                j = i - 1
                if 0 <= j < NT:
                    xt, bxt, hb, bhb, s_, bs_ = live[j]
                    kb.op("vector", lambda e: e.scalar_tensor_tensor(
                        out=hb[:], in0=xt[:], scalar=s_[:, 2:3], in1=gt_ffn[:], op0=ALU.mult, op1=ALU.mult),
                        reads=[bxt, bs_, bgf], writes=[bhb])
                    p, bp = pT.next()
                    for k in range(8):
                        kb.op("tensor", lambda e, k=k: e.transpose(out=p[:, k * 128:(k + 1) * 128],
                                                                   in_=hb[:, k * 128:(k + 1) * 128], identity=ident[:]),
                              reads=[bhb, bid], writes=[bp] if k == 0 else (), pwrites=[bp] if k > 0 else (), track=(k == 7))
                    live[j] = (p, bp)
                j = i - 2
                if 0 <= j < NT:
                    p, bp = live.pop(j)
                    kb.op("scalar", lambda e: e.copy(out=hT[:, :, j * 128:(j + 1) * 128],
                                                     in_=p[:, 0:1024].rearrange("p (k t) -> p k t", k=8)),
                          reads=[bp], writes=[bhT[j]])

        aT = H.sb("aT", [128, 22, 2048], BF16)
        with kb.scope() as S2:
            slab_t = [S2.sb("slab%d" % i, [128, 8, 512], BF16) for i in range(3)]
            slabs = Rot(list(zip(slab_t, kb.bufs("slab", 3))))
            sg_t = [S2.sb("sg%d" % i, [128, 512], F32) for i in range(2)]
            sgs = Rot(list(zip(sg_t, kb.bufs("sg", 2))))
            w_in_v = w_in.rearrange("(kc p) n -> p kc n", p=128)
            NSL = DFF // 256

            def load_slab(f):
                sl, bsl = slabs.next()
                kb.dma("gpsimd", sl[:, :, 0:256], w_in_v[:, :, f * 256:(f + 1) * 256], bsl, writes=[bsl])
                kb.dma("gpsimd", sl[:, :, 256:512], w_in_v[:, :, DFF + f * 256:DFF + (f + 1) * 256], bsl, pwrites=[bsl])
                return sl, bsl

            pend = [load_slab(0), load_slab(1)]
            for f in range(NSL):
                sl, bsl = pend.pop(0)
                if f + 2 < NSL:
                    pend.append(load_slab(f + 2))
                for tb in range(4):
                    rb = bhT[tb * 4:(tb + 1) * 4]
                    for j in range(2):
                        ffc = f * 2 + j
                        pg, bpg = pf.next()
                        pu, bpu = pf.next()
                        for (pp, bpp, c0) in ((pg, bpg, j * 128), (pu, bpu, 256 + j * 128)):
                            for kc in range(8):
                                kb.op("tensor", lambda e, kc=kc, pp=pp, c0=c0, sl=sl, tb=tb: e.matmul(
                                    out=pp[:], lhsT=sl[:, kc, c0:c0 + 128], rhs=hT[:, kc, tb * 512:(tb + 1) * 512],
                                    start=(kc == 0), stop=(kc == 7)),
                                    reads=[bsl] + rb, writes=[bpp] if kc == 0 else (), pwrites=[bpp] if kc > 0 else (),
                                    track=(kc == 7))
                        sg, bsg = sgs.next()
                        kb.op("scalar", lambda e, sg=sg, pg=pg: e.activation(out=sg[:], in_=pg[:], func=AF.Silu),
                              reads=[bpg], writes=[bsg])
                        kb.op("vector", lambda e, sg=sg, pu=pu, ffc=ffc, tb=tb: e.tensor_tensor(
                            out=aT[:, ffc, tb * 512:(tb + 1) * 512], in0=sg[:], in1=pu[:], op=ALU.mult),
                            reads=[bsg, bpu], pwrites=[baT[tb]])

        with kb.scope() as S3:
            wout = S3.sb("wout", [128, 22, D], BF16); bwout = kb.buf("wout")
            w_out_v = w_out.rearrange("(kc p) n -> p kc n", p=128)
            for k0 in range(0, 22, 2):
                kb.dma("gpsimd", wout[:, k0:k0 + 2, :], w_out_v[:, k0:k0 + 2, :], bwout, pwrites=[bwout])
            xt_t = [S3.sb("xu%d" % i, [128, D], F32) for i in range(4)]
            xts = Rot(list(zip(xt_t, kb.bufs("xu", 4))))
            hb_t = [S3.sb("hc%d" % i, [128, D], BF16) for i in range(2)]
            hbs = Rot(list(zip(hb_t, kb.bufs("hc", 2))))
            if final:
                ot_t = [S3.sb("ot%d" % i, [128, D], F32) for i in range(2)]
                ots = Rot(list(zip(ot_t, kb.bufs("ot", 2))))
            else:
                stg_t = [S3.sb("stg%d" % i, [128, 8, 512], BF16) for i in range(1)] * 2
                bstg = kb.bufs("stg", 1) * 2
                hT_out_v = None if fused else hT_out.rearrange("(kc p) t -> p kc t", p=128)

            def load_x1(t):
                xt, bxt = xts.next()
                kb.dma("sync", xt[:], x1s[t * 128:(t + 1) * 128, :], bxt, reads=[bx1[t]], writes=[bxt])
                return xt, bxt

            def down_proj(t, xt, bxt):
                tb = t // 4
                for c in range(2):
                    ps, bps = pf.next()
                    for kc in range(22):
                        kb.op("tensor", lambda e, kc=kc, c=c, ps=ps: e.matmul(
                            out=ps[:], lhsT=aT[:, kc, t * 128:(t + 1) * 128], rhs=wout[:, kc, c * 512:(c + 1) * 512],
                            start=(kc == 0), stop=(kc == 21)),
                            reads=[baT[tb], bwout], writes=[bps] if kc == 0 else (), pwrites=[bps] if kc > 0 else (),
                            track=(kc == 21))
                    kb.op("vector", lambda e, c=c, ps=ps: e.tensor_tensor(
                        out=xt[:, c * 512:(c + 1) * 512], in0=xt[:, c * 512:(c + 1) * 512], in1=ps[:], op=ALU.add),
                        reads=[bps] + ([bxt] if c == 0 else []), writes=[bxt] if c == 0 else (),
                        pwrites=[bxt] if c == 1 else ())
                hb, bhb = hbs.next()
                s_, bs_ = st.next()
                emit_rstd(kb, xt[:], bxt, s_, bs_, hb[:], bhb, epst[:, 0:1], beps)
                return hb, bhb, s_, bs_

            def epilogue(t, xt, bxt, hb, bhb, s_, bs_):
                tb = t // 4
                if final:
                    ot, bot = ots.next()
                    kb.op("vector", lambda e: e.scalar_tensor_tensor(
                        out=ot[:], in0=xt[:], scalar=s_[:, 2:3], in1=gt_next[:], op0=ALU.mult, op1=ALU.mult),
                        reads=[bxt, bs_, bgn], writes=[bot])
                    kb.dma("sync", out[t * 128:(t + 1) * 128, :], ot[:], bot, reads=[bot], pwrites=[bouts[t]])
                else:
                    kb.dma("sync", x_out[t * 128:(t + 1) * 128, :], xt[:], bxt, reads=[bxt], pwrites=[bouts[t]])
                    kb.op("vector", lambda e: e.scalar_tensor_tensor(
                        out=hb[:], in0=xt[:], scalar=s_[:, 2:3], in1=gt_next[:], op0=ALU.mult, op1=ALU.mult),
                        reads=[bxt, bs_, bgn], writes=[bhb])
                    p, bp = pT.next()
                    sg_, bsg_ = stg_t[tb % 2], bstg[tb % 2]
                    emit_transpose8(kb, hb, bhb, p, bp, ident[:], bid, sg_[:, :, (t % 4) * 128:(t % 4 + 1) * 128], bsg_,
                                    dst_pw=(t % 4 != 0))
                    if t % 4 == 3:
                        if fused:
                            kb.dma("sync", T["hT_out4"][tb].rearrange("(kc p) t -> p kc t", p=128), sg_[:], bsg_,
                                   reads=[bsg_], pwrites=[bouts[20 + tb]])
                            T["after_blk"](tb, [bouts[20 + tb]])
                        else:
                            kb.dma("sync", hT_out_v[:, :, tb * 512:(tb + 1) * 512], sg_[:], bsg_, reads=[bsg_],
                                   pwrites=[bouts[20 + tb]])

            pend = [load_x1(0), load_x1(1)]
            prev = None
            for t in range(NT):
                xt, bxt = pend.pop(0)
                if t + 2 < NT:
                    pend.append(load_x1(t + 2))
                cur = (t, xt, bxt) + down_proj(t, xt, bxt)
                if prev is not None:
                    epilogue(*prev)
                prev = cur
            epilogue(*prev)
        kb.finish(bouts)
        H.stack.close()
    if fused:
        return bouts
    kb.close()
    return nc


_CACHE = {}
_TIMES = []


def _run(nc, in_maps):
    import os
    if os.environ.get("K_TRACE"):
        res = run_bass_kernel_spmd(nc, in_maps, core_ids=list(range(NCORES)), trace=True)
        _TIMES.append(res.exec_time_ns)
        print("exec_time_ns", res.exec_time_ns, flush=True)
        return res
    return run_bass_kernel_spmd(nc, in_maps, core_ids=list(range(NCORES)))


def _get(name, fn):
    if name not in _CACHE:
        _CACHE[name] = fn()
    return _CACHE[name]


def _bcast(v):
    return np.ascontiguousarray(np.broadcast_to(np.asarray(v, np.float32)[None, :], (128, v.shape[0])))


def run_ffn_phase(final, inT_list, x_list, w_o, w_in, w_out, g_ffn, g_next):
    KIN = w_o.shape[0]
    nc = _get(("ffn", KIN, final), lambda: build_ffn(KIN, final))
    ident = np.eye(128, dtype=np.float32)
    gf, gn = _bcast(g_ffn), _bcast(g_next)
    in_maps = []
    for c in range(NCORES):
        in_maps.append({"inT": inT_list[c], "x": x_list[c], "w_o": w_o, "w_in": w_in, "w_out": w_out,
                        "g_ffn": gf, "g_next": gn, "ident": ident})
    res = _run(nc, in_maps)
    return res.results


def build_fox(nc=None, kb=None, T=None):
    fused = T is not None
    TT = T
    if not fused:
        nc = bass.Bass("TRN2", target_bir_lowering=False)
    dt = (lambda n, s, d, k: T[n]) if fused else (lambda n, s, d, k: nc.dram_tensor(n, s, d, kind=k).ap())
    x = dt("x", [S, D], F32, "ExternalInput")
    wqk = dt("wqk", [D, 512], F32, "ExternalInput")
    wvf = dt("wvf", [D, 320], F32, "ExternalInput")
    nbfd = dt("nbf", [128, 2], F32, "ExternalInput")
    g_mix = dt("g_mix", [128, D], F32, "ExternalInput")
    identd = dt("ident", [128, 128], F32, "ExternalInput")
    trid = dt("tri", [128, 128], F32, "ExternalInput")
    onesd = dt("ones", [128, 128], F32, "ExternalInput")
    maskd = dt("maskneg", [128, 128], F32, "ExternalInput")
    oT = dt("oT", [256, S], BF16, "ExternalOutput")
    import os
    STOP = int(os.environ.get("FOX_STOP", "9"))
    SUB = int(os.environ.get("FOX_SUB", "9"))
    VB = int(os.environ.get("FOX_V", "3"))
    NT = S // 128
    NQB = S // 512
    scale = 128.0 ** -0.5

    if not fused:
        kb = KB(nc)
    bouts = kb.bufs("o", 2 * NQB)
    with kb.scope(barrier=fused) as P:
        epst = P.sb("epst", [128, 1], F32); beps = kb.buf("eps")
        kb.op("vector", lambda e: e.memset(epst[:], EPS), writes=[beps])
        gt = P.sb("gt", [128, D], F32); bgt = kb.buf("gt")
        ident = P.sb("identb", [128, 128], BF16); bid = kb.buf("id")
        maskn = P.sb("maskn", [128, 128], BF16); bmk = kb.buf("mk")
        tri = P.sb("tri_sb", [128, 128], F32); btri = kb.buf("tri")
        ones = P.sb("ones_sb", [128, 128], F32); bones = kb.buf("ones")
        nbf = P.sb("nbf_sb", [128, 2], F32); bnbf = kb.buf("nbf")
        kb.dma("sync", gt[:], g_mix[:, :], bgt, writes=[bgt])
        kb.dma("gpsimd", ident[:], identd[:, :], bid, writes=[bid])
        kb.dma("gpsimd", maskn[:], maskd[:, :], bmk, writes=[bmk])
        kb.dma("sync", tri[:], trid[:, :], btri, writes=[btri])
        kb.dma("sync", ones[:], onesd[:, :], bones, writes=[bones])
        kb.dma("sync", nbf[:], nbfd[:, :], bnbf, writes=[bnbf])
        st_t = [P.sb("st%d" % i, [128, 4], F32) for i in range(4)]
        st = Rot(list(zip(st_t, kb.bufs("st", 4))))
        QT = [P.sb("QT%d" % h, [128, S], BF16) for h in range(2)]
        KT = [P.sb("KT%d" % h, [128, S], BF16) for h in range(2)]
        bQT = [kb.bufs("QT%d_" % h, NQB) for h in range(2)]
        bKT = [kb.bufs("KT%d_" % h, NQB) for h in range(2)]
        VP = P.sb("VP", [128, NT, 2, 132], BF16); bVP = kb.bufs("VP", NT)
        Fl = P.sb("Fl", [128, NT, 2], F32); bF = kb.buf("F")
        CSP = P.sb("CSP", [128, NT, 2], F32); bCSP = kb.buf("CSP")
        PINC = P.sb("PINC", [128, NT, 2], F32); bPINC = kb.buf("PINC")
        bones_col = kb.buf("onescol")
        kb.op("vector", lambda e: e.memset(VP[:, :, :, 128:129], 1.0), writes=[bones_col])

        for S1 in scope_if(kb, STOP >= 1):
            wqk_sb = S1.sb("wqk_sb", [128, 8, 512], BF16); bwqk = kb.buf("wqk")
            wvf_sb = S1.sb("wvf_sb", [128, 8, 320], BF16); bwvf = kb.buf("wvf")
            kb.dma("gpsimd", wqk_sb[:], wqk.rearrange("(kc p) n -> p kc n", p=128), bwqk, writes=[bwqk])
            kb.dma("gpsimd", wvf_sb[:], wvf.rearrange("(kc p) n -> p kc n", p=128), bwvf, writes=[bwvf])
            xt_t = [S1.sb("xt%d" % i, [128, D], F32) for i in range(5)]
            xts = Rot(list(zip(xt_t, kb.bufs("xt", 5))))
            hb_t = [S1.sb("hb%d" % i, [128, D], BF16) for i in range(3)]
            hbs = Rot(list(zip(hb_t, kb.bufs("hb", 3))))
            hTb_t = [S1.sb("hTb%d" % i, [128, 8, 512], BF16) for i in range(2)]
            bhTb = [kb.bufs("hTb%d_" % i, 4) for i in range(2)]
            pT_t = [S1.ps("pT%d" % i, [128, 1024], BF16) for i in range(2)]
            pT = Rot(list(zip(pT_t, kb.bufs("pT", 2))))
            pf_t = [S1.ps("pf%d" % i, [128, 512], F32) for i in range(6)]
            pf = Rot(list(zip(pf_t, kb.bufs("pf", 6))))

            def load_x(t):
                xt, bxt = xts.next()
                kb.dma("sync", xt[:], x[t * 128:(t + 1) * 128, :], bxt, writes=[bxt])
                return xt, bxt

            def blk_matmuls(blk):
                hTb, bh = hTb_t[blk % 2], bhTb[blk % 2]
                for idx in range(4 if SUB >= 2 else 0):
                    ps, bps = pf.next()
                    for kc in range(8):
                        kb.op("tensor", lambda e, kc=kc, ps=ps, idx=idx: e.matmul(
                            out=ps[:], lhsT=wqk_sb[:, kc, idx * 128:(idx + 1) * 128], rhs=hTb[:, kc, :],
                            start=(kc == 0), stop=(kc == 7)),
                            reads=[bwqk] + bh, writes=[bps] if kc == 0 else (), pwrites=[bps] if kc > 0 else (),
                            track=(kc == 7))
                    dstT, bd = (QT, bQT) if idx < 2 else (KT, bKT)
                    h = idx % 2
                    kb.op("scalar", lambda e, ps=ps, dstT=dstT, h=h: e.copy(
                        out=dstT[h][:, blk * 512:(blk + 1) * 512], in_=ps[:]), reads=[bps], writes=[bd[h][blk]])
                for ti in range(4 if SUB >= 3 else 0):
                    t = blk * 4 + ti
                    ps, bps = pf.next()
                    for kc in range(8):
                        kb.op("tensor", lambda e, kc=kc, ps=ps, ti=ti: e.matmul(
                            out=ps[:, 0:320], lhsT=hTb[:, kc, ti * 128:(ti + 1) * 128], rhs=wvf_sb[:, kc, :],
                            start=(kc == 0), stop=(kc == 7)),
                            reads=[bwvf, bh[ti]], writes=[bps] if kc == 0 else (), pwrites=[bps] if kc > 0 else (),
                            track=(kc == 7))
                    kb.op("vector", lambda e, ps=ps, t=t: e.tensor_copy(
                        out=VP[:, t, :, 0:128], in_=ps[:, 0:256].rearrange("p (h d) -> p h d", h=2)),
                        reads=[bps], writes=[bVP[t]])
                    kb.op("vector", lambda e, ps=ps, t=t: e.tensor_copy(out=Fl[:, t, :], in_=ps[:, 256:258]),
                          reads=[bps], pwrites=[bF])

            pend = [load_x(0), load_x(1), load_x(2)]
            live = {}
            for i in range(NT + 2):
                if i < NT:
                    xt, bxt = pend.pop(0)
                    if i + 3 < NT:
                        pend.append(load_x(i + 3))
                    hb, bhb = hbs.next()
                    s_, bs_ = st.next()
                    emit_rstd(kb, xt[:], bxt, s_, bs_, hb[:], bhb, epst[:, 0:1], beps)
                    live[i] = (xt, bxt, hb, bhb, s_, bs_)
                j = i - 1
                if 0 <= j < NT:
                    xt, bxt, hb, bhb, s_, bs_ = live[j]
                    kb.op("vector", lambda e: e.scalar_tensor_tensor(
                        out=hb[:], in0=xt[:], scalar=s_[:, 2:3], in1=gt[:], op0=ALU.mult, op1=ALU.mult),
                        reads=[bxt, bs_, bgt], writes=[bhb])
                    p, bp = pT.next()
                    for k in range(8):
                        kb.op("tensor", lambda e, k=k: e.transpose(out=p[:, k * 128:(k + 1) * 128],
                                                                   in_=hb[:, k * 128:(k + 1) * 128], identity=ident[:]),
                              reads=[bhb, bid], writes=[bp] if k == 0 else (), pwrites=[bp] if k > 0 else (), track=(k == 7))
                    live[j] = (p, bp)
                j = i - 2
                if 0 <= j < NT:
                    p, bp = live.pop(j)
                    blk, ti = j // 4, j % 4
                    hTb, bh = hTb_t[blk % 2], bhTb[blk % 2]
                    kb.op("scalar", lambda e: e.copy(out=hTb[:, :, ti * 128:(ti + 1) * 128],
                                                     in_=p[:, 0:1024].rearrange("p (k t) -> p k t", k=8)),
                          reads=[bp], writes=[bh[ti]])
                    if ti == 3:
                        blk_matmuls(blk)

        bdbg = []
        for S2 in scope_if(kb, STOP >= 2):
            E = S2.sb("E", [128, NT, 2], F32); bE = kb.buf("E")
            SP = S2.sb("SP", [128, NT, 2], F32); bSP = kb.buf("SP")
            Tt = S2.sb("Tt", [128, NT, 2], F32); bTt = kb.buf("Tt")
            TM = S2.sb("TM", [128, NT, 2], F32); bTM = kb.buf("TM")
            onec = S2.sb("onec", [128, NT], F32); bonec = kb.buf("onec")
            psW = S2.ps("psW", [128, 128], F32); bpsW = kb.buf("psW")
            psT = S2.ps("psT", [128, 128], F32); bpsT = kb.buf("psT")
            kb.op("vector", lambda e: e.memset(onec[:], 1.0), writes=[bonec])
            for h in range(2):
                kb.op("scalar", lambda e, h=h: e.activation(out=E[:, :, h], in_=Fl[:, :, h], func=AF.Exp,
                                                            bias=nbf[:, h:h + 1], scale=-1.0),
                      reads=[bF, bnbf], pwrites=[bE])
            kb.op("scalar", lambda e: e.activation(out=SP[:], in_=E[:], func=AF.Ln, bias=onec[:, 0:1], scale=1.0),
                  reads=[bE, bonec], writes=[bSP])
            spf = SP[:].rearrange("p t h -> p (t h)")
            kb.op("tensor", lambda e: e.matmul(out=psW[:], lhsT=tri[:], rhs=spf, start=True, stop=True),
                  reads=[btri, bSP], writes=[bpsW])
            kb.op("tensor", lambda e: e.matmul(out=psT[:], lhsT=ones[:], rhs=spf, start=True, stop=True),
                  reads=[bones, bSP], writes=[bpsT])
            kb.op("vector", lambda e: e.tensor_copy(out=Tt[:].rearrange("p t h -> p (t h)"), in_=psT[:]),
                  reads=[bpsT], writes=[bTt])
            for h in range(2):
                kb.op("vector", lambda e, h=h: e.tensor_tensor_scan(out=PINC[:, :, h], data0=onec[:], data1=Tt[:, :, h],
                                                                    initial=0.0, op0=ALU.mult, op1=ALU.add),
                      reads=[bonec, bTt], pwrites=[bPINC])
            kb.op("vector", lambda e: e.tensor_tensor(out=TM[:], in0=PINC[:], in1=Tt[:], op=ALU.subtract),
                  reads=[bPINC, bTt], writes=[bTM])
            kb.op("vector", lambda e: e.tensor_tensor(out=CSP[:].rearrange("p t h -> p (t h)"), in0=psW[:],
                                                      in1=TM[:].rearrange("p t h -> p (t h)"), op=ALU.add),
                  reads=[bpsW, bTM], writes=[bCSP])

        for S3 in scope_if(kb, STOP >= 3):
            pS_t = [S3.ps("pS%d" % i, [128, 2, 512], F32) for i in range(2)]
            pS = Rot(list(zip(pS_t, kb.bufs("pS", 2))))
            pO_t = [S3.ps("pO%d" % i, [128, 512], F32) for i in range(2)]
            bpO = kb.bufs("pO", 2)
            pR = S3.ps("pR", [128, 512], F32); bpR = kb.buf("pR")
            onesb = S3.sb("onesb", [128, 128], BF16); bonesb = kb.buf("onesb")
            kb.op("vector", lambda e: e.memset(onesb[:], 1.0), writes=[bonesb])
            PT_t = [S3.sb("PT%d" % i, [128, 2, 512], BF16) for i in range(3)]
            PTs = Rot(list(zip(PT_t, kb.bufs("PT", 3))))
            bq_t = [S3.sb("bq%d" % i, [128, NT], F32) for i in range(2)]
            bbqs = kb.bufs("bq", 2)
            Pend = S3.sb("Pend", [128, NT, 2], F32); bPend = kb.buf("Pend")
            wexp = S3.sb("wexp", [128, NT, 2], F32); bwexp = kb.buf("wexp")
            WB = S3.sb("WB", [128, NT, 2, 128], BF16); bWB = kb.buf("WB")
            pinc_pairs = PINC[:].rearrange("p (a two) h -> p a two h", two=2)
            pend_pairs = Pend[:].rearrange("p (a two) h -> p a two h", two=2)
            for two in range(2):
                kb.op("vector", lambda e, two=two: e.tensor_copy(out=pend_pairs[:, :, two, :], in_=pinc_pairs[:, :, 1, :]),
                      reads=[bPINC], writes=[bPend] if two == 0 else (), pwrites=[bPend] if two == 1 else ())
            kb.op("vector", lambda e: e.tensor_tensor(out=wexp[:], in0=CSP[:], in1=Pend[:], op=ALU.subtract),
                  reads=[bCSP, bPend], writes=[bwexp])
            kb.op("scalar", lambda e: e.activation(out=wexp[:], in_=wexp[:], func=AF.Exp), reads=[bwexp], writes=[bwexp])
            for kt in range(NT):
                for h in range(2):
                    kb.op("vector", lambda e, kt=kt, h=h: e.tensor_scalar_mul(
                        out=VP[:, kt, h, 0:128], in0=VP[:, kt, h, 0:128], scalar1=wexp[:, kt, h:h + 1]),
                        reads=[bwexp, bVP[kt]], writes=[bVP[kt]])
                    kb.op("vector", lambda e, kt=kt, h=h: e.tensor_scalar_mul(
                        out=WB[:, kt, h, :], in0=onesb[:], scalar1=wexp[:, kt, h:h + 1]),
                        reads=[bwexp, bonesb], pwrites=[bWB])
            ssA_t = [S3.sb("ssA%d" % i, [128, 512], F32) for i in range(2)]; bssA = kb.bufs("ssA", 2)
            rs_t = [S3.sb("rs%d" % i, [128, 512], F32) for i in range(2)]
            rss = Rot(list(zip(rs_t, kb.bufs("rs", 2))))
            os_t = [S3.sb("os%d" % i, [128, 512], BF16) for i in range(2)]
            oss = Rot(list(zip(os_t, kb.bufs("os", 2))))
            blocks = [(h, qb) for h in range(2) for qb in range(NQB)]

            def emit_bias(i):
                h, qb = blocks[i]
                nk = 4 * qb + 4
                bq, bbq = bq_t[i % 2], bbqs[i % 2]
                kb.op("vector", lambda e: e.tensor_scalar(
                    out=bq[:, 0:nk], in0=Pend[:, 0:nk, h], scalar1=PINC[:, nk - 3, h:h + 1], scalar2=None,
                    op0=ALU.subtract), reads=[bPend, bPINC], writes=[bbq])

            units = []
            for i, (h, qb) in enumerate(blocks):
                for k0 in range(0, 4 * qb, 2):
                    units.append((i, [k0, k0 + 1]))
                for kt in range(4 * qb, 4 * qb + 4):
                    units.append((i, [kt]))

            def emit_qk(u):
                i, kts = units[u]
                h, qb = blocks[i]
                ps, bps = pS.next()
                for ui, kt in enumerate(kts):
                    j0 = max(0, kt - 4 * qb)
                    c0 = j0 * 128
                    diag = kt >= 4 * qb
                    kb.op("tensor", lambda e: e.matmul(
                        out=ps[:, ui, c0:512], lhsT=KT[h][:, kt * 128:(kt + 1) * 128],
                        rhs=QT[h][:, qb * 512 + c0:(qb + 1) * 512], start=True, stop=(not diag)),
                        reads=[bKT[h][kt // 4], bQT[h][qb]], writes=[bps] if ui == 0 else (),
                        pwrites=[bps] if ui > 0 else (), track=(not diag and ui == len(kts) - 1))
                    if diag:
                        kb.op("tensor", lambda e: e.matmul(
                            out=ps[:, ui, c0:c0 + 128], lhsT=ident[:], rhs=maskn[:], start=False, stop=True),
                            reads=[bid, bmk], pwrites=[bps])
                return ps, bps

            emit_bias(0)
            nxt = emit_qk(0)
            for u, (i, kts) in enumerate(units):
                h, qb = blocks[i]
                nk = 4 * qb + 4
                ps, bps = nxt
                if kts[0] == 0 and i + 1 < len(blocks):
                    emit_bias(i + 1)
                if u + 1 < len(units):
                    nxt = emit_qk(u + 1)
                bq, bbq = bq_t[i % 2], bbqs[i % 2]
                pO, bO = pO_t[i % 2], bpO[i % 2]
                ssA, bsA = ssA_t[i % 2], bssA[i % 2]
                pt, bpt = PTs.next()
                if len(kts) == 2:
                    kb.op("scalar", lambda e: e.activation(
                        out=pt[:].rearrange("p a b -> p (a b)"), in_=ps[:].rearrange("p a b -> p (a b)"),
                        func=AF.Exp, bias=bq[:, kts[0]:kts[0] + 1], scale=scale), reads=[bps, bbq], writes=[bpt])
                else:
                    c0 = max(0, kts[0] - 4 * qb) * 128
                    kb.op("scalar", lambda e: e.activation(
                        out=pt[:, 0, c0:512], in_=ps[:, 0, c0:512], func=AF.Exp, bias=bq[:, kts[0]:kts[0] + 1], scale=scale),
                        reads=[bps, bbq], writes=[bpt])
                for ui, kt in enumerate(kts):
                    c0 = max(0, kt - 4 * qb) * 128
                    kb.op("tensor", lambda e: e.matmul(
                        out=pO[:, c0:512], lhsT=VP[:, kt, h, 0:128], rhs=pt[:, ui, c0:512], start=(kt == 0), stop=(kt == nk - 1),
                        skip_group_check=True),
                        reads=[bpt, bVP[kt]], writes=[bO] if kt == 0 else (), pwrites=[bO] if kt > 0 else ())
                    if kt == 0:
                        kb.op("vector", lambda e: e.tensor_scalar_mul(out=ssA[:], in0=pt[:, ui, :], scalar1=wexp[:, 0, h:h + 1]),
                              reads=[bpt, bwexp], writes=[bsA])
                    elif kt % 2 == 1:
                        first_pe = (kt == 1)
                        kb.op("tensor", lambda e: e.matmul(out=pR[:, c0:512], lhsT=WB[:, kt, h, :], rhs=pt[:, ui, c0:512],
                                                           start=first_pe, stop=False, skip_group_check=True),
                              reads=[bpt, bWB], writes=[bpR] if first_pe else (), pwrites=() if first_pe else [bpR])
                    else:
                        kb.op("vector", lambda e: e.scalar_tensor_tensor(
                            out=ssA[:, c0:512], in0=pt[:, ui, c0:512], scalar=wexp[:, kt, h:h + 1], in1=ssA[:, c0:512],
                            op0=ALU.mult, op1=ALU.add), reads=[bpt, bwexp, bsA], writes=[bsA])
                    if kt == nk - 1:
                        kb.op("tensor", lambda e: e.matmul(out=pR[:], lhsT=ones[:], rhs=ssA[:], start=False, stop=True,
                                                           skip_group_check=True),
                              reads=[bones, bsA], pwrites=[bpR])
                        rs, brs = rss.next()
                        kb.op("vector", lambda e: e.reciprocal(out=rs[:], in_=pR[:]), reads=[bpR], writes=[brs])
                        osb, bos = oss.next()
                        kb.op("vector", lambda e: e.tensor_tensor(out=osb[:], in0=pO[:], in1=rs[:], op=ALU.mult),
                              reads=[bO, brs], writes=[bos])
                        odst = (T["oT4"][qb, h * 128:(h + 1) * 128, :] if fused
                                else oT[h * 128:(h + 1) * 128, qb * 512:(qb + 1) * 512])
                        kb.dma("sync", odst, osb[:], bos, reads=[bos], pwrites=[bouts[i]])
                        if fused and h == 1 and "after_shard" in TT:
                            TT["after_shard"](qb, [bouts[qb], bouts[NQB + qb]])
        kb.finish(bouts + bdbg)
    if fused:
        return bouts
    kb.close()
    return nc


def run_fox_phase(x, fox_w_in, fox_b_f, g_mix):
    nc = _get("fox", build_fox)
    ident = np.eye(128, dtype=np.float32)
    tri = np.triu(np.ones((128, 128), np.float32))
    ones = np.ones((128, 128), np.float32)
    kk = np.arange(128)
    maskneg = np.where(kk[:, None] > kk[None, :], NEG, 0.0).astype(np.float32)
    gm = _bcast(g_mix)
    in_maps = []
    for c in range(NCORES):
        b, hp = c // 4, c % 4
        h0, h1 = 2 * hp, 2 * hp + 1
        cols_qk = np.concatenate([np.arange(h0 * 128, h0 * 128 + 128), np.arange(h1 * 128, h1 * 128 + 128),
                                  D + np.arange(h0 * 128, h0 * 128 + 128), D + np.arange(h1 * 128, h1 * 128 + 128)])
        cols_vf = np.concatenate([2 * D + np.arange(h0 * 128, h0 * 128 + 128), 2 * D + np.arange(h1 * 128, h1 * 128 + 128),
                                  np.array([3 * D + h0, 3 * D + h1])])
        wvf_p = np.zeros((D, 320), np.float32)
        wvf_p[:, :258] = fox_w_in[:, cols_vf]
        nbf = np.ascontiguousarray(np.broadcast_to(-fox_b_f[[h0, h1]][None, :], (128, 2))).astype(np.float32)
        in_maps.append({"x": np.ascontiguousarray(x[b]), "wqk": np.ascontiguousarray(fox_w_in[:, cols_qk]),
                        "wvf": wvf_p, "nbf": nbf, "g_mix": gm,
                        "ident": ident, "tri": tri, "ones": ones, "maskneg": maskneg})
    res = _run(nc, in_maps)
    return res.results


def build_ret(nc=None, kb=None, T=None):
    fused = T is not None
    TT = T
    if not fused:
        nc = bass.Bass("TRN2", target_bir_lowering=False)
    dt = (lambda n, s, d, k: T[n]) if fused else (lambda n, s, d, k: nc.dram_tensor(n, s, d, kind=k).ap())
    h1T = dt("h1T", [4, D, 2048], BF16, "ExternalInput")
    wqk = dt("wqk", [D, 512], F32, "ExternalInput")
    wv = dt("wv", [D, 512], F32, "ExternalInput")
    wg = dt("wg", [D, 512], F32, "ExternalInput")
    cosd = dt("cosT", [128, S], F32, "ExternalInput")
    sind = dt("sinT", [128, S], F32, "ExternalInput")
    qdecd = dt("qdec", [128, 512], F32, "ExternalInput")
    cstd = dt("cst", [128, 2], F32, "ExternalInput")
    dmaskd = dt("dmask", [128, 64], F32, "ExternalInput")
    identd = dt("ident", [128, 128], F32, "ExternalInput")
    ygT = dt("ygT", [512, S], BF16, "ExternalOutput")
    NBLK = S // 512

    if not fused:
        kb = KB(nc)
    bouts = kb.bufs("o", NBLK)
    with kb.scope(barrier=fused) as P:
        ident = P.sb("identb", [128, 128], BF16); bid = kb.buf("id")
        epst = P.sb("epst", [128, 1], F32); beps = kb.buf("eps")
        kb.op("vector", lambda e: e.memset(epst[:], EPS), writes=[beps])
        qdec = P.sb("qdec_sb", [128, 512], F32); bqdec = kb.buf("qdec")
        cst = P.sb("cst_sb", [128, 2], F32); bcst = kb.buf("cst")
        dmask = P.sb("dmask_sb", [128, 64], F32); bdm = kb.buf("dm")
        wqk_sb = P.sb("wqk_sb", [128, 8, 512], BF16); bwqk = kb.buf("wqk")
        wv_sb = P.sb("wv_sb", [128, 8, 512], BF16); bwv = kb.buf("wv")
        wg_sb = P.sb("wg_sb", [128, 8, 512], BF16); bwg = kb.buf("wg")
        kb.dma("gpsimd", ident[:], identd[:, :], bid, writes=[bid])
        kb.dma("sync", qdec[:], qdecd[:, :], bqdec, writes=[bqdec])
        kb.dma("sync", cst[:], cstd[:, :], bcst, writes=[bcst])
        kb.dma("sync", dmask[:], dmaskd[:, :], bdm, writes=[bdm])
        for (wsb, wd, bw) in ((wqk_sb, wqk, bwqk), (wv_sb, wv, bwv), (wg_sb, wg, bwg)):
            kb.dma("gpsimd", wsb[:], wd.rearrange("(kc p) n -> p kc n", p=128), bw, writes=[bw])
        Sf = P.sb("Sf", [128, 2, 512], F32); bSf = kb.buf("Sf")
        Sb_t = [P.sb("Sb%d" % i, [128, 2, 512], BF16) for i in range(2)]
        bSb = kb.bufs("Sb", 2)
        kb.op("vector", lambda e: e.memset(Sf[:], 0.0), writes=[bSf])
        SF_INIT_DONE = True
        kb.op("vector", lambda e: e.memset(Sb_t[0][:], 0.0), writes=[bSb[0]])
        hbl_t = [P.sb("hbl%d" % i, [128, 8, 512], BF16) for i in range(2)]
        bhbl = kb.bufs("hbl", 2)
        cs_t = [P.sb("cs%d" % i, [128, 2, 512], F32) for i in range(2)]
        bcs = kb.bufs("cs", 2)
        ksb = P.sb("ksb", [128, 2, 512], F32); bksb = kb.buf("ksb")
        tq_t = [P.sb("tq%d" % i, [128, 512], F32) for i in range(2)]; btq = kb.bufs("tq", 2)
        tk_t = [P.sb("tk%d" % i, [128, 512], F32) for i in range(2)]; btk = kb.bufs("tk", 2)
        QR_t = [P.sb("QR%d" % i, [128, 2, 512], BF16) for i in range(2)]; bQR = kb.bufs("QR", 2)
        QD_t = [P.sb("QD%d" % i, [128, 2, 512], BF16) for i in range(2)]; bQD = kb.bufs("QD", 2)
        KR_t = [P.sb("KR%d" % i, [128, 2, 512], BF16) for i in range(2)]; bKR = kb.bufs("KR", 2)
        KD_t = [P.sb("KD%d" % i, [128, 256], BF16) for i in range(3)]
        KDs = Rot(list(zip(KD_t, kb.bufs("KD", 3))))
        V_t = [P.sb("V%d" % i, [128, 512], BF16) for i in range(3)]
        Vs = Rot(list(zip(V_t, kb.bufs("V", 3))))
        SG_t = [P.sb("SG%d" % i, [128, 512], F32) for i in range(3)]
        SGs = Rot(list(zip(SG_t, kb.bufs("SG", 3))))
        aT_t = [P.sb("aTs%d" % i, [128, 64], BF16) for i in range(2)]
        aTs = Rot(list(zip(aT_t, kb.bufs("aTs", 2))))
        bn_t = [P.sb("bn%d" % i, [128, 12], F32) for i in range(2)]
        bns = Rot(list(zip(bn_t, kb.bufs("bn", 2))))
        yn_t = [P.sb("yn%d" % i, [128, 512], F32) for i in range(2)]
        yns = Rot(list(zip(yn_t, kb.bufs("yn", 2))))
        yg_t = [P.sb("yg%d" % i, [128, 512], BF16) for i in range(2)]
        ygs = Rot(list(zip(yg_t, kb.bufs("yg", 2))))
        stg_t = [P.sb("stg%d" % i, [128, 4, 512], BF16) for i in range(2)]
        bstg = kb.bufs("stg", 2)
        pB = P.ps("pBt", [128, 1024], BF16); bpB = kb.bufs("pB", 2)
        pA = P.ps("pAt", [128, 512], F32); bpA = kb.bufs("pA", 2)
        pY_t = [P.ps("pY%d" % i, [128, 512], F32) for i in range(2)]; bpY = kb.bufs("pY", 2)
        pF_t = [P.ps("pF%d" % i, [128, 512], F32) for i in range(4)]
        pF = Rot(list(zip(pF_t, kb.bufs("pF", 4))))
        qs = P.sb("qs", [128, 2, 512], F32); bqs = kb.bufs("qs", 2)
        SG4_t = [P.sb("SG4_%d" % i, [128, 512], F32) for i in range(4)]
        SG4 = list(zip(SG4_t, kb.bufs("SG4", 4)))
        V4_t = [P.sb("V4_%d" % i, [128, 512], BF16) for i in range(4)]
        V4 = list(zip(V4_t, kb.bufs("V4", 4)))
        KD4_t = [P.sb("KD4_%d" % i, [128, 256], BF16) for i in range(4)]
        KD4 = list(zip(KD4_t, kb.bufs("KD4", 4)))
        aT4_t = [P.sb("aT4_%d" % i, [128, 64], BF16) for i in range(4)]
        aT4 = list(zip(aT4_t, kb.bufs("aT4", 4)))

        def mm_group(ps, bps, pairs, reads):
            n = len(pairs)
            for i, (l, r) in enumerate(pairs):
                kb.op("tensor", lambda e, l=l, r=r, i=i: e.matmul(out=ps, lhsT=l, rhs=r, start=(i == 0), stop=(i == n - 1)),
                      reads=reads, writes=[bps] if i == 0 else (), pwrites=[bps] if i > 0 else (), track=(i == n - 1))

        def load_blk(blk):
            r, off = blk // 4, (blk % 4) * 512
            hb, bh = hbl_t[blk % 2], bhbl[blk % 2]
            if fused:
                kb.dma("sync", hb[:], h1T[blk % 4, r].rearrange("(kc p) t -> p kc t", p=128), bh,
                       reads=TT.get("in_deps", []), writes=[bh])
            else:
                kb.dma("sync", hb[:], h1T[r].rearrange("(kc p) t -> p kc t", p=128)[:, :, off:off + 512], bh, writes=[bh])
            cs, bc = cs_t[blk % 2], bcs[blk % 2]
            kb.dma("sync", cs[:, 0, :], cosd[:, blk * 512:(blk + 1) * 512], bc, writes=[bc])
            kb.dma("sync", cs[:, 1, :], sind[:, blk * 512:(blk + 1) * 512], bc, pwrites=[bc])

        def blk_prep(blk):
            hb, bh = hbl_t[blk % 2], bhbl[blk % 2]
            cs, bc = cs_t[blk % 2], bcs[blk % 2]
            QR, bqr = QR_t[blk % 2], bQR[blk % 2]
            QD, bqd = QD_t[blk % 2], bQD[blk % 2]
            KR, bkr = KR_t[blk % 2], bKR[blk % 2]
            cos, sin = cs[:, 0, :], cs[:, 1, :]
            for idx in range(4):
                ps, bps = pF.next()
                mm_group(ps[:], bps, [(wqk_sb[:, kc, idx * 128:(idx + 1) * 128], hb[:, kc, :]) for kc in range(8)],
                         [bwqk, bh])
                dst, bd = (qs, bqs) if idx < 2 else (ksb, None)
                if idx < 2:
                    kb.op("scalar", lambda e, ps=ps, idx=idx: e.copy(out=qs[:, idx, :], in_=ps[:]), reads=[bps], writes=[bqs[idx]])
                else:
                    kb.op("scalar", lambda e, ps=ps, idx=idx: e.copy(out=ksb[:, idx - 2, :], in_=ps[:]), reads=[bps],
                          writes=[bksb] if idx == 2 else (), pwrites=[bksb] if idx == 3 else ())
            tA, btA, tB, btB = tq_t[0], btq[0], tq_t[1], btq[1]
            V_ = "vector"
            kb.op(V_, lambda e: e.tensor_tensor(out=tA[:], in0=qs[:, 0, :], in1=cos, op=ALU.mult), reads=[bqs[0], bc], writes=[btA])
            kb.op(V_, lambda e: e.tensor_tensor(out=tB[:], in0=qs[:, 1, :], in1=sin, op=ALU.mult), reads=[bqs[1], bc], writes=[btB])
            kb.op(V_, lambda e: e.tensor_tensor(out=QR[:, 0, :], in0=tA[:], in1=tB[:], op=ALU.subtract),
                  reads=[btA, btB], writes=[bqr])
            kb.op(V_, lambda e: e.tensor_tensor(out=tA[:], in0=qs[:, 0, :], in1=sin, op=ALU.mult), reads=[bqs[0], bc], writes=[btA])
            kb.op(V_, lambda e: e.tensor_tensor(out=tB[:], in0=qs[:, 1, :], in1=cos, op=ALU.mult), reads=[bqs[1], bc], writes=[btB])
            kb.op(V_, lambda e: e.tensor_tensor(out=QR[:, 1, :], in0=tA[:], in1=tB[:], op=ALU.add),
                  reads=[btA, btB], pwrites=[bqr])
            for dc in range(2):
                kb.op(V_, lambda e, dc=dc: e.tensor_tensor(out=QD[:, dc, :], in0=QR[:, dc, :], in1=qdec[:], op=ALU.mult),
                      reads=[bqr, bqdec], writes=[bqd] if dc == 0 else (), pwrites=[bqd] if dc == 1 else ())
            tA, btA, tB, btB = tk_t[0], btk[0], tk_t[1], btk[1]
            G_ = "vector"
            kb.op(G_, lambda e: e.tensor_tensor(out=tA[:], in0=ksb[:, 0, :], in1=cos, op=ALU.mult), reads=[bksb, bc], writes=[btA])
            kb.op(G_, lambda e: e.tensor_tensor(out=tB[:], in0=ksb[:, 1, :], in1=sin, op=ALU.mult), reads=[bksb, bc], writes=[btB])
            kb.op(G_, lambda e: e.tensor_tensor(out=KR[:, 0, :], in0=tA[:], in1=tB[:], op=ALU.subtract),
                  reads=[btA, btB], writes=[bkr])
            kb.op(G_, lambda e: e.tensor_tensor(out=tA[:], in0=ksb[:, 0, :], in1=sin, op=ALU.mult), reads=[bksb, bc], writes=[btA])
            kb.op(G_, lambda e: e.tensor_tensor(out=tB[:], in0=ksb[:, 1, :], in1=cos, op=ALU.mult), reads=[bksb, bc], writes=[btB])
            kb.op(G_, lambda e: e.tensor_tensor(out=KR[:, 1, :], in0=tA[:], in1=tB[:], op=ALU.add),
                  reads=[btA, btB], pwrites=[bkr])

        def tile_prep(T):
            blk, ti = T // 4, T % 4
            tc0 = ti * 128
            hb, bh = hbl_t[blk % 2], bhbl[blk % 2]
            QR, bqr = QR_t[blk % 2], bQR[blk % 2]
            KR, bkr = KR_t[blk % 2], bKR[blk % 2]
            V, bV = V4[T % 4]
            SG, bSG = SG4[T % 4]
            KD, bKD = KD4[T % 4]
            aTb, baT = aT4[T % 4]
            pv, bpv = pF.next()
            mm_group(pv[:], bpv, [(hb[:, kc, tc0:tc0 + 128], wv_sb[:, kc, :]) for kc in range(8)], [bwv, bh])
            kb.op("scalar", lambda e: e.copy(out=V[:], in_=pv[:]), reads=[bpv], writes=[bV])
            pg, bpg = pF.next()
            mm_group(pg[:], bpg, [(hb[:, kc, tc0:tc0 + 128], wg_sb[:, kc, :]) for kc in range(8)], [bwg, bh])
            kb.op("scalar", lambda e: e.activation(out=SG[:], in_=pg[:], func=AF.Silu), reads=[bpg], writes=[bSG])
            for dc in range(2):
                kb.op("tensor", lambda e, dc=dc: e.transpose(
                    out=pB[:, dc * 128:(dc + 1) * 128], in_=KR[:, dc, tc0:tc0 + 128], identity=ident[:]),
                    reads=[bkr, bid], writes=[bpB[0]] if dc == 0 else (), pwrites=[bpB[0]] if dc == 1 else (), track=(dc == 1))
            kb.op("scalar", lambda e: e.activation(out=KD[:], in_=pB[:, 0:256], func=AF.Copy, scale=cst[:, 0:1]),
                  reads=[bpB[0], bcst], writes=[bKD])
            a0 = (T % 2) * 64
            for c in range(2):
                t0 = tc0 + c * 64
                for dc in range(2):
                    kb.op("tensor", lambda e, c=c, dc=dc, t0=t0: e.matmul(
                        out=pA[c * 64:(c + 1) * 64, a0:a0 + 64], lhsT=KR[:, dc, t0:t0 + 64], rhs=QR[:, dc, t0:t0 + 64],
                        start=(dc == 0), stop=(dc == 1), skip_group_check=True),
                        reads=[bkr, bqr], writes=[bpA[T % 2]] if (c == 0 and dc == 0) else (),
                        pwrites=() if (c == 0 and dc == 0) else [bpA[T % 2]], track=(c == 1 and dc == 1))
            kb.op("vector", lambda e: e.tensor_tensor(out=aTb[:], in0=pA[:, a0:a0 + 64], in1=dmask[:], op=ALU.mult),
                  reads=[bpA[T % 2], bdm], writes=[baT])

        state = {"cur": 0}

        def chunk(T, c, part, sidx=None):
            blk, ti = T // 4, T % 4
            tc0 = ti * 128
            lo, hi = c * 64, (c + 1) * 64
            t0 = tc0 + c * 64
            QD, bqd = QD_t[blk % 2], bQD[blk % 2]
            V, bV = V4[T % 4]
            KD, bKD = KD4[T % 4]
            aTb, baT = aT4[T % 4]
            py, bpy = pY_t[T % 2], bpY[T % 2]
            if part == 1:
                Sb, bsb = Sb_t[sidx], bSb[sidx]
                kb.op("tensor", lambda e: e.matmul(
                    out=py[lo:hi, :], lhsT=aTb[lo:hi, 0:64], rhs=V[lo:hi, :], start=True, stop=False),
                    reads=[baT, bV], writes=[bpy] if c == 0 else (), pwrites=[bpy] if c == 1 else (), track=False)
                for dc in range(2):
                    kb.op("tensor", lambda e, dc=dc: e.matmul(
                        out=py[lo:hi, :], lhsT=QD[:, dc, t0:t0 + 64], rhs=Sb[:, dc, :], start=False, stop=(dc == 1)),
                        reads=[bqd, bsb], pwrites=[bpy], track=(dc == 1))
                return
            pU = [pF.next(), pF.next()]
            for dc in range(2):
                pu, bpu = pU[dc]
                kb.op("tensor", lambda e, dc=dc, pu=pu: e.matmul(
                    out=pu[:], lhsT=KD[lo:hi, dc * 128:(dc + 1) * 128], rhs=V[lo:hi, :], start=True, stop=True),
                    reads=[bKD, bV], writes=[bpu])
            return pU

        def state_update(pU):
            nxt = 1 - state["cur"]
            for dc in range(2):
                pu, bpu = pU[dc]
                kb.op("vector", lambda e, dc=dc, pu=pu: e.scalar_tensor_tensor(
                    out=Sf[:, dc, :], in0=Sf[:, dc, :], scalar=cst[:, 1:2], in1=pu[:], op0=ALU.mult, op1=ALU.add),
                    reads=[bpu, bcst, bSfd[dc]], writes=[bSfd[dc]])
                kb.op("scalar", lambda e, dc=dc: e.copy(out=Sb_t[nxt][:, dc, :], in_=Sf[:, dc, :]), reads=[bSfd[dc]],
                      writes=[bSb[nxt]] if dc == 0 else (), pwrites=[bSb[nxt]] if dc == 1 else ())
            state["cur"] = nxt

        def post_a(T):
            py, bpy = pY_t[T % 2], bpY[T % 2]
            bn, bbn = bn_t[T % 2], bbns[T % 2]
            kb.op("vector", lambda e: e.bn_stats(out=bn[:, 0:6], in_=py[:]), reads=[bpy], writes=[bbn])
            kb.op("vector", lambda e: e.bn_aggr(out=bn[:, 6:8], in_=bn[:, 0:6]), reads=[bbn], writes=[bbn])
            kb.op("scalar", lambda e: e.activation(out=bn[:, 8:9], in_=bn[:, 7:8], func=AF.Ln, bias=epst[:, 0:1], scale=1.0),
                  reads=[bbn, beps], writes=[bbn])
            kb.op("scalar", lambda e: e.activation(out=bn[:, 9:10], in_=bn[:, 8:9], func=AF.Exp, scale=-0.5),
                  reads=[bbn], writes=[bbn])

        def post_b(T):
            blk, ti = T // 4, T % 4
            tc0 = ti * 128
            py, bpy = pY_t[T % 2], bpY[T % 2]
            bn, bbn = bn_t[T % 2], bbns[T % 2]
            SG, bSG = SG4[T % 4]
            stg, bsg = stg_t[blk % 2], bstg[blk % 2]
            yn, byn = yns.next()
            kb.op("vector", lambda e: e.tensor_scalar(
                out=yn[:], in0=py[:], scalar1=bn[:, 6:7], scalar2=bn[:, 9:10], op0=ALU.subtract, op1=ALU.mult),
                reads=[bpy, bbn], writes=[byn])
            yg, byg = ygs.next()
            kb.op("vector", lambda e: e.tensor_tensor(out=yg[:], in0=yn[:], in1=SG[:], op=ALU.mult),
                  reads=[byn, bSG], writes=[byg])
            for fc in range(4):
                kb.op("tensor", lambda e, fc=fc: e.transpose(
                    out=pB[:, 512 + fc * 128:512 + (fc + 1) * 128], in_=yg[:, fc * 128:(fc + 1) * 128], identity=ident[:]),
                    reads=[byg, bid], writes=[bpB[1]] if fc == 0 else (), pwrites=[bpB[1]] if fc > 0 else (), track=(fc == 3))
            kb.op("scalar", lambda e: e.copy(
                out=stg[:, :, tc0:tc0 + 128], in_=pB[:, 512:1024].rearrange("p (f t) -> p f t", f=4)),
                reads=[bpB[1]], writes=[bsg] if ti == 0 else (), pwrites=[bsg] if ti > 0 else ())
            if ti == 3:
                ydst = (TT["ygT4"][blk].rearrange("(fc p) t -> p fc t", p=128)
                        if fused else ygT.rearrange("(fc p) t -> p fc t", p=128)[:, :, blk * 512:(blk + 1) * 512])
                kb.dma("sync", ydst, stg[:], bsg, reads=[bsg], pwrites=[bouts[blk]])
                if fused and "after_shard" in TT:
                    TT["after_shard"](blk, [bouts[blk]])

        bbns = kb.bufs("bnb", 2)
        byns = kb.bufs("ynb", 2)
        bSfd = kb.bufs("Sfd", 2)
        NTT = S // 128
        load_blk(0)
        load_blk(1)
        blk_prep(0)
        tile_prep(0)
        for T in range(NTT):
            blk, ti = T // 4, T % 4
            uA = chunk(T, 0, 0)
            uB = chunk(T, 1, 0)
            chunk(T, 0, 1, state["cur"])
            state_update(uA)
            idxA = state["cur"]
            state_update(uB)
            if T + 1 < NTT:
                if (T + 1) % 4 == 0:
                    nb = (T + 1) // 4
                    if nb + 1 < NBLK:
                        load_blk(nb + 1)
                if (T + 2) % 4 == 0 and (T + 2) // 4 < NBLK:
                    blk_prep((T + 2) // 4)
                tile_prep(T + 1)
            chunk(T, 1, 1, idxA)
            post_a(T)
            if T >= 1:
                post_b(T - 1)
        post_b(NTT - 1)
        kb.finish(bouts)
    if fused:
        return bouts
    kb.close()
    return nc


def _ret_consts():
    half = 128
    inv = (10000.0 ** (-np.arange(half, dtype=np.float32) / half)).astype(np.float32)
    ang = (np.arange(S, dtype=np.float32)[:, None] * inv[None, :]).astype(np.float32)
    cosT = np.ascontiguousarray(np.cos(ang).T.astype(np.float32))
    sinT = np.ascontiguousarray(np.sin(ang).T.astype(np.float32))
    return cosT, sinT


def run_ret_phase(h1T_full, ret_w_in):
    nc = _get("ret", build_ret)
    ident = np.eye(128, dtype=np.float32)
    cosT, sinT = _get("retc", _ret_consts)
    idx = np.arange(64, dtype=np.float32)
    in_maps = []
    for c in range(NCORES):
        b, hd = c // 4, c % 4
        lg = np.log(np.float32(1.0 - 2.0 ** (-5.0 - hd))).astype(np.float32)
        qd = np.exp(lg * (idx + 1.0)).astype(np.float32)
        kd = np.exp(lg * (63.0 - idx)).astype(np.float32) * np.float32(256.0 ** -0.5)
        cd = np.exp(lg * 64.0).astype(np.float32)
        dm = (np.exp(lg * np.abs(idx[:, None] - idx[None, :])) * np.float32(256.0 ** -0.5)).astype(np.float32)
        qdec = np.ascontiguousarray(np.broadcast_to(np.tile(qd, 8)[None, :], (128, 512))).astype(np.float32)
        cst = np.stack([np.tile(kd, 2), np.full(128, cd, np.float32)], axis=1).astype(np.float32)
        dmask = np.concatenate([dm, dm], axis=0).astype(np.float32)
        wqk = np.concatenate([ret_w_in[:, hd * 256:(hd + 1) * 256], ret_w_in[:, D + hd * 256:D + (hd + 1) * 256]], axis=1)
        wv = ret_w_in[:, 2 * D + hd * 512:2 * D + (hd + 1) * 512]
        wg = ret_w_in[:, 4 * D + hd * 512:4 * D + (hd + 1) * 512]
        in_maps.append({"h1T": h1T_full[b], "wqk": np.ascontiguousarray(wqk), "wv": np.ascontiguousarray(wv),
                        "wg": np.ascontiguousarray(wg), "cosT": cosT, "sinT": sinT, "qdec": qdec, "cst": cst,
                        "dmask": dmask, "ident": ident})
    res = _run(nc, in_maps)
    return res.results


def build_fused():
    nc = bass.Bass("TRN2", target_bir_lowering=False)
    EI = lambda n, s, d=F32: nc.dram_tensor(n, s, d, kind="ExternalInput").ap()
    IN = lambda n, s, d: nc.dram_tensor(n, s, d, kind="Internal").ap()
    kb = KB(nc)
    rg = [[0, 1, 2, 3], [4, 5, 6, 7]]
    j = nc.partition_id() % 4
    ident = EI("ident", [128, 128])
    oT4 = IN("oT4", [16, 256, 512], BF16)
    oT_g = IN("oT_g", [16, 4, 256, 512], BF16)
    TA = {"x": EI("x", [S, D]), "wqk": EI("a_wqk", [D, 512]), "wvf": EI("a_wvf", [D, 320]), "nbf": EI("a_nbf", [128, 2]),
          "g_mix": EI("a_gmix", [128, D]), "ident": ident, "tri": EI("tri", [128, 128]), "ones": EI("ones", [128, 128]),
          "maskneg": EI("maskneg", [128, 128]), "oT": None, "oT4": oT4}
    kb.pfx = "A_"
    bcA, bcB, bcC = kb.buf("cA"), kb.buf("cB"), kb.buf("cC")
    for b_ in (bcA, bcB, bcC):
        kb.mkdma(b_)
    kb.async_bufs = [bcA, bcB, bcC]

    def shard_done_A(a, deps):
        kb.coll("AllGather", rg, oT4[a], oT_g[a].rearrange("r f t -> (r f) t"), bcA, reads=deps, pwrites=[bcA])

    TA["after_shard"] = shard_done_A
    build_fox(nc, kb, TA)
    x2s = IN("x2s", [2048, D], F32)
    h1T_loc = IN("h1T_loc", [4, D, 512], BF16)
    h1T_g = IN("h1T_g", [4, 4, D, 512], BF16)
    TB = {"inT": None, "inT_dyn": (oT_g, j, "sync"), "in_deps": [bcA], "x": EI("xs", [2048, D]),
          "w_o": EI("b_wo", [D, D]), "w_in": EI("b_win", [D, 2 * DFF]), "w_out": EI("b_wout", [DFF, D]),
          "g_ffn": EI("b_gffn", [128, D]), "g_next": EI("b_gnext", [128, D]), "ident": ident,
          "x_out": x2s, "hT_out": None, "hT_out4": h1T_loc, "x1s": IN("x1s_b", [2048, D], F32)}
    kb.pfx = "B_"

    def blk_done_B(tb, deps):
        kb.coll("AllGather", rg, h1T_loc[tb], h1T_g[tb].rearrange("r f t -> (r f) t"), bcB, reads=deps, pwrites=[bcB])

    TB["after_blk"] = blk_done_B
    build_ffn(D, False, nc, kb, TB)
    ygT4 = IN("ygT4", [16, 512, 512], BF16)
    ygT_g = IN("ygT_g", [16, 4, 512, 512], BF16)
    TC = {"in_deps": [bcB], "h1T": h1T_g, "wqk": EI("c_wqk", [D, 512]), "wv": EI("c_wv", [D, 512]), "wg": EI("c_wg", [D, 512]),
          "cosT": EI("cosT", [128, S]), "sinT": EI("sinT", [128, S]), "qdec": EI("qdec", [128, 512]),
          "cst": EI("cst", [128, 2]), "dmask": EI("dmask", [128, 64]), "ident": ident, "ygT": None, "ygT4": ygT4}
    kb.pfx = "C_"

    def shard_done_C(a, deps):
        kb.coll("AllGather", rg, ygT4[a], ygT_g[a].rearrange("r f t -> (r f) t"), bcC, reads=deps, pwrites=[bcC])

    TC["after_shard"] = shard_done_C
    build_ret(nc, kb, TC)
    TD = {"inT": None, "inT_dyn": (ygT_g, j, "scalar"), "in_deps": [bcC], "x": x2s,
          "w_o": EI("d_wo", [2 * D, D]), "w_in": EI("d_win", [D, 2 * DFF]), "w_out": EI("d_wout", [DFF, D]),
          "g_ffn": EI("d_gffn", [128, D]), "g_next": EI("d_gnext", [128, D]), "ident": ident,
          "out": nc.dram_tensor("out", [2048, D], F32, kind="ExternalOutput").ap(), "x1s": IN("x1s_d", [2048, D], F32)}
    kb.pfx = "D_"
    build_ffn(2 * D, True, nc, kb, TD)
    kb.close()
    return nc


def _fox_core_inputs(c, fox_w_in, fox_b_f):
    hp = c % 4
    h0, h1 = 2 * hp, 2 * hp + 1
    cols_qk = np.concatenate([np.arange(h0 * 128, h0 * 128 + 128), np.arange(h1 * 128, h1 * 128 + 128),
                              D + np.arange(h0 * 128, h0 * 128 + 128), D + np.arange(h1 * 128, h1 * 128 + 128)])
    cols_vf = np.concatenate([2 * D + np.arange(h0 * 128, h0 * 128 + 128), 2 * D + np.arange(h1 * 128, h1 * 128 + 128),
                              np.array([3 * D + h0, 3 * D + h1])])
    wvf_p = np.zeros((D, 320), np.float32)
    wvf_p[:, :258] = fox_w_in[:, cols_vf]
    nbf = np.ascontiguousarray(np.broadcast_to(-fox_b_f[[h0, h1]][None, :], (128, 2))).astype(np.float32)
    return np.ascontiguousarray(fox_w_in[:, cols_qk]), wvf_p, nbf


def _ret_core_inputs(c, ret_w_in):
    hd = c % 4
    idx = np.arange(64, dtype=np.float32)
    lg = np.log(np.float32(1.0 - 2.0 ** (-5.0 - hd))).astype(np.float32)
    qd = np.exp(lg * (idx + 1.0)).astype(np.float32)
    kd = np.exp(lg * (63.0 - idx)).astype(np.float32) * np.float32(256.0 ** -0.5)
    cd = np.exp(lg * 64.0).astype(np.float32)
    dm = (np.exp(lg * np.abs(idx[:, None] - idx[None, :])) * np.float32(256.0 ** -0.5)).astype(np.float32)
    qdec = np.ascontiguousarray(np.broadcast_to(np.tile(qd, 8)[None, :], (128, 512))).astype(np.float32)
    cst = np.stack([np.tile(kd, 2), np.full(128, cd, np.float32)], axis=1).astype(np.float32)
    dmask = np.concatenate([dm, dm], axis=0).astype(np.float32)
    wqk = np.concatenate([ret_w_in[:, hd * 256:(hd + 1) * 256], ret_w_in[:, D + hd * 256:D + (hd + 1) * 256]], axis=1)
    wv = ret_w_in[:, 2 * D + hd * 512:2 * D + (hd + 1) * 512]
    wg = ret_w_in[:, 4 * D + hd * 512:4 * D + (hd + 1) * 512]
    return (np.ascontiguousarray(wqk), np.ascontiguousarray(wv), np.ascontiguousarray(wg), qdec, cst, dmask)


def kernel_fused(x, norm_mix, norm_ffn, fox_w_in, fox_b_f, fox_w_out, ret_w_in, ret_w_out, ffn_w_in, ffn_w_out, final_norm):
    nc = _get("fused", build_fused)
    ident = np.eye(128, dtype=np.float32)
    tri = np.triu(np.ones((128, 128), np.float32))
    ones = np.ones((128, 128), np.float32)
    kk = np.arange(128)
    maskneg = np.where(kk[:, None] > kk[None, :], NEG, 0.0).astype(np.float32)
    cosT, sinT = _get("retc", _ret_consts)
    in_maps = []
    for c in range(NCORES):
        b, jj = c // 4, c % 4
        a_wqk, a_wvf, a_nbf = _fox_core_inputs(c, fox_w_in[0], fox_b_f[0])
        c_wqk, c_wv, c_wg, qdec, cst, dmask = _ret_core_inputs(c, ret_w_in[0])
        in_maps.append({
            "ident": ident, "tri": tri, "ones": ones, "maskneg": maskneg,
            "x": np.ascontiguousarray(x[b]), "xs": np.ascontiguousarray(x[b, jj * 2048:(jj + 1) * 2048, :]),
            "a_wqk": a_wqk, "a_wvf": a_wvf, "a_nbf": a_nbf, "a_gmix": _bcast(norm_mix[0]),
            "b_wo": fox_w_out[0], "b_win": ffn_w_in[0], "b_wout": ffn_w_out[0],
            "b_gffn": _bcast(norm_ffn[0]), "b_gnext": _bcast(norm_mix[1]),
            "c_wqk": c_wqk, "c_wv": c_wv, "c_wg": c_wg, "cosT": cosT, "sinT": sinT, "qdec": qdec, "cst": cst, "dmask": dmask,
            "d_wo": ret_w_out[0], "d_win": ffn_w_in[1], "d_wout": ffn_w_out[1],
            "d_gffn": _bcast(norm_ffn[1]), "d_gnext": _bcast(final_norm),
        })
    res = _run(nc, in_maps).results
    out = np.empty((NB, S, D), np.float32)
    for c in range(NCORES):
        out[c // 4, (c % 4) * 2048:(c % 4 + 1) * 2048, :] = res[c]["out"]
    return out


FUSED = True


def kernel(x, norm_mix, norm_ffn, fox_w_in, fox_b_f, fox_w_out, ret_w_in, ret_w_out, ffn_w_in, ffn_w_out, final_norm):
    x = np.asarray(x, np.float32)
    f32 = lambda a: np.asarray(a, np.float32)
    norm_mix, norm_ffn, final_norm = f32(norm_mix), f32(norm_ffn), f32(final_norm)
    fox_w_in, fox_b_f, fox_w_out = f32(fox_w_in), f32(fox_b_f), f32(fox_w_out)
    ret_w_in, ret_w_out, ffn_w_in, ffn_w_out = f32(ret_w_in), f32(ret_w_out), f32(ffn_w_in), f32(ffn_w_out)
    if FUSED:
        return kernel_fused(x, norm_mix, norm_ffn, fox_w_in, fox_b_f, fox_w_out, ret_w_in, ret_w_out, ffn_w_in,
                            ffn_w_out, final_norm)
    resA = run_fox_phase(x, fox_w_in[0], fox_b_f[0], norm_mix[0])
    oT_full = [np.concatenate([resA[b * 4 + j]["oT"] for j in range(4)], axis=0) for b in range(NB)]
    inT = [np.ascontiguousarray(oT_full[c // 4][:, (c % 4) * 2048:(c % 4 + 1) * 2048]) for c in range(NCORES)]
    xs = [np.ascontiguousarray(x[c // 4, (c % 4) * 2048:(c % 4 + 1) * 2048, :]) for c in range(NCORES)]
    resB = run_ffn_phase(False, inT, xs, fox_w_out[0], ffn_w_in[0], ffn_w_out[0], norm_ffn[0], norm_mix[1])
    h1T_full = [np.stack([resB[b * 4 + j]["hT_out"] for j in range(4)], axis=0) for b in range(NB)]
    x2 = [resB[c]["x_out"] for c in range(NCORES)]
    resC = run_ret_phase(h1T_full, ret_w_in[0])
    ygT_full = [np.concatenate([resC[b * 4 + j]["ygT"] for j in range(4)], axis=0) for b in range(NB)]
    inT = [np.ascontiguousarray(ygT_full[c // 4][:, (c % 4) * 2048:(c % 4 + 1) * 2048]) for c in range(NCORES)]
    resD = run_ffn_phase(True, inT, x2, ret_w_out[0], ffn_w_in[1], ffn_w_out[1], norm_ffn[1], final_norm)
    out = np.empty((NB, S, D), np.float32)
    for c in range(NCORES):
        out[c // 4, (c % 4) * 2048:(c % 4 + 1) * 2048, :] = resD[c]["out"]
    return out
```
